# Optimizing a Trainium2 kernel written in Bass

```python
import jax, jax.numpy as jnp
from jax import lax
import numpy as np

D_MODEL = 2048
BATCH = 1
SEQ = 16384
DEPTH = 1

D_FF = 5632
RMS_EPS = 1e-6
ROPE_THETA = 500000.0
A_HEAD_DIM = 128
A_HEADS = D_MODEL // (2 * A_HEAD_DIM)
A_WIDTH = A_HEADS * A_HEAD_DIM
ROPE_DIM = A_HEAD_DIM // 4
DILATED_PATTERNS = ((128, 1), (512, 4), (2048, 16))
A_BLOCK = 128
MAX_DILATION = 16
B_HEADS = 4
B_VAL_DIM = D_MODEL // (2 * B_HEADS)
B_KEY_DIM = B_VAL_DIM // 2
B_KEY_WIDTH = B_HEADS * B_KEY_DIM
B_VAL_WIDTH = B_HEADS * B_VAL_DIM
GATE_RANK = 16
GATE_NORMALIZER = 16.0
GLA_CHUNK = 64
GLA_SUB = 16
IN_SPLITS = (A_WIDTH, A_WIDTH, A_WIDTH,
             B_KEY_WIDTH, B_KEY_WIDTH, B_VAL_WIDTH, B_VAL_WIDTH, GATE_RANK,
             D_MODEL, D_MODEL)
IN_WIDTH = sum(IN_SPLITS)

kernel_name = "hybrid_dilated_gla_macaron_block"


def rms_norm(x, g):
    xf = x.astype(jnp.float32)
    y = xf * lax.rsqrt(jnp.mean(xf * xf, axis=-1, keepdims=True) + RMS_EPS)
    return (y * g.astype(jnp.float32)).astype(x.dtype)


def swiglu(h, w_gate, w_up, w_down):
    return (jax.nn.silu(h @ w_gate) * (h @ w_up)) @ w_down


def partial_rope(t, positions):
    half = ROPE_DIM // 2
    inv = jnp.power(ROPE_THETA, -(jnp.arange(half, dtype=jnp.float32) * 2.0 / ROPE_DIM))
    ang = positions.astype(jnp.float32)[..., None] * inv
    cos = jnp.cos(ang)[:, :, None, :]
    sin = jnp.sin(ang)[:, :, None, :]
    t1 = t[..., :half].astype(jnp.float32)
    t2 = t[..., half:ROPE_DIM].astype(jnp.float32)
    rot = jnp.concatenate([t1 * cos - t2 * sin, t2 * cos + t1 * sin], axis=-1).astype(t.dtype)
    return jnp.concatenate([rot, t[..., ROPE_DIM:]], axis=-1)


def dilated_pattern(q, k, v, window, dilation):
    B, S_pad, H, dh = q.shape
    span = window // dilation
    M = S_pad // dilation
    nb = M // A_BLOCK

    def to_blocks(t):
        return t.reshape(B, M, dilation, H, dh).transpose(0, 2, 3, 1, 4).reshape(
            B, dilation, H, nb, A_BLOCK, dh)

    def with_prev(t):
        prev = jnp.pad(t[:, :, :, :-1], ((0, 0), (0, 0), (0, 0), (1, 0), (0, 0), (0, 0)))
        return jnp.concatenate([prev, t], axis=4)

    qb = to_blocks(q)
    kb = with_prev(to_blocks(k))
    vb = with_prev(to_blocks(v))
    s = jnp.einsum('brhnqe,brhnke->brhnqk', qb, kb).astype(jnp.float32) * (dh ** -0.5)
    qi = jnp.arange(A_BLOCK)[:, None]
    kj = jnp.arange(2 * A_BLOCK)[None, :]
    dist = qi + A_BLOCK - kj
    band = (dist >= 0) & (dist <= span)
    blk = jnp.arange(nb)[:, None, None]
    valid = band[None] & ((blk > 0) | (kj[None] >= A_BLOCK))
    s = jnp.where(valid, s, -jnp.inf)
    lse = jax.nn.logsumexp(s, axis=-1)
    p = jnp.exp(s - lse[..., None]).astype(v.dtype)
    o = jnp.einsum('brhnqk,brhnke->brhnqe', p, vb)
    o = o.reshape(B, dilation, H, M, dh).transpose(0, 3, 1, 2, 4).reshape(B, S_pad, H, dh)
    lse = lse.reshape(B, dilation, H, M).transpose(0, 3, 1, 2).reshape(B, S_pad, H)
    return o, lse


def dilated_attention(q, k, v):
    B, S, H, dh = q.shape
    unit = A_BLOCK * MAX_DILATION
    S_pad = -(-S // unit) * unit
    pad = ((0, 0), (0, S_pad - S), (0, 0), (0, 0))
    qp, kp, vp = jnp.pad(q, pad), jnp.pad(k, pad), jnp.pad(v, pad)
    outs, lses = [], []
    for window, dilation in DILATED_PATTERNS:
        o_i, lse_i = dilated_pattern(qp, kp, vp, window, dilation)
        outs.append(o_i)
        lses.append(lse_i)
    w = jax.nn.softmax(jnp.stack(lses, axis=0), axis=0)
    o = jnp.sum(w[..., None] * jnp.stack(outs, axis=0).astype(jnp.float32), axis=0)
    return o[:, :S].astype(q.dtype)


def gla_chunked(q, k, v, log_a):
    out_dtype = v.dtype
    B, S, H, dk = q.shape
    dv = v.shape[-1]
    C, Cs = GLA_CHUNK, GLA_SUB
    NS = C // Cs
    n = S // C

    def chunks(t):
        return t.astype(jnp.float32).reshape(B, n, C, H, t.shape[-1]).transpose(0, 3, 1, 2, 4)

    q, k, v, g = chunks(q), chunks(k), chunks(v), chunks(log_a)
    b = jnp.cumsum(g, axis=3)
    b_last = b[:, :, :, -1]
    k_to_end = k * jnp.exp(b_last[:, :, :, None] - b)
    upd = jnp.einsum('bhncd,bhnce->bhnde', k_to_end, v)

    def step(state, inp):
        decay, u = inp
        return decay[..., None] * state + u, state

    s0 = jnp.zeros((B, H, dk, dv), jnp.float32)
    _, s_prev = lax.scan(step, s0, (jnp.moveaxis(jnp.exp(b_last), 2, 0), jnp.moveaxis(upd, 2, 0)))
    s_prev = jnp.moveaxis(s_prev, 0, 2)
    o_inter = jnp.einsum('bhncd,bhnde->bhnce', q * jnp.exp(b), s_prev)
    qs = q.reshape(B, H, n, NS, Cs, dk)
    ksub = k.reshape(B, H, n, NS, Cs, dk)
    vs = v.reshape(B, H, n, NS, Cs, dv)
    bs = b.reshape(B, H, n, NS, Cs, dk)
    ref = jnp.concatenate([jnp.zeros_like(bs[:, :, :, :1, 0]), bs[:, :, :, :-1, -1]], axis=3)
    q_ref = qs * jnp.exp(bs - ref[:, :, :, :, None])
    k_ref = k[:, :, :, None] * jnp.exp(jnp.minimum(ref[:, :, :, :, None] - b[:, :, :, None], 0.0))
    a_cross = jnp.einsum('bhnsid,bhnsjd->bhnsij', q_ref, k_ref)
    cross_mask = jnp.arange(C)[None, None, :] < (jnp.arange(NS) * Cs)[:, None, None]
    a_cross = jnp.where(cross_mask, a_cross, 0.0)
    o_cross = jnp.einsum('bhnsij,bhnje->bhnsie', a_cross, v)
    tri = jnp.tril(jnp.ones((Cs, Cs), dtype=bool))
    diff = bs[:, :, :, :, :, None, :] - bs[:, :, :, :, None, :, :]
    decay = jnp.exp(jnp.where(tri[:, :, None], diff, -jnp.inf))
    a_diag = jnp.einsum('bhnsid,bhnsjd,bhnsijd->bhnsij', qs, ksub, decay)
    o_diag = jnp.einsum('bhnsij,bhnsje->bhnsie', a_diag, vs)
    o = o_inter + (o_cross + o_diag).reshape(B, H, n, C, dv)
    return o.transpose(0, 2, 3, 1, 4).reshape(B, S, H, dv).astype(out_dtype)


def setup_inputs(seed: int = 0) -> dict:
    key = jax.random.key(seed)
    ks = jax.random.split(key, 21)
    f32 = jnp.float32
    L = DEPTH

    def dense(k, shape, fan_in):
        return jax.random.normal(k, shape, f32) * (fan_in ** -0.5)

    def gain(k, dim):
        return 1.0 + 0.02 * jax.random.normal(k, (L, dim), f32)

    return {
        "x": jax.random.normal(ks[0], (BATCH, SEQ, D_MODEL), f32),
        "positions": jnp.broadcast_to(jnp.arange(SEQ, dtype=jnp.int32), (BATCH, SEQ)),
        "ffn1_norm": gain(ks[1], D_MODEL),
        "ffn1_w_gate": dense(ks[2], (L, D_MODEL, D_FF), D_MODEL),
        "ffn1_w_up": dense(ks[3], (L, D_MODEL, D_FF), D_MODEL),
        "ffn1_w_down": dense(ks[4], (L, D_FF, D_MODEL), D_FF),
        "mix_norm": gain(ks[5], D_MODEL),
        "w_in": dense(ks[6], (L, D_MODEL, IN_WIDTH), D_MODEL),
        "a_q_norm": gain(ks[7], A_HEAD_DIM),
        "a_k_norm": gain(ks[8], A_HEAD_DIM),
        "b_gate_w2": dense(ks[9], (L, GATE_RANK, B_KEY_WIDTH), GATE_RANK),
        "b_gate_bias": 0.1 * jax.random.normal(ks[10], (L, B_KEY_WIDTH), f32),
        "b_out_norm": gain(ks[11], B_VAL_DIM),
        "w_a_up": dense(ks[12], (L, A_WIDTH, D_MODEL), A_WIDTH),
        "w_b_up": dense(ks[13], (L, B_VAL_WIDTH, D_MODEL), B_VAL_WIDTH),
        "w_out": dense(ks[14], (L, D_MODEL, D_MODEL), D_MODEL),
        "ffn2_norm": gain(ks[15], D_MODEL),
        "ffn2_w_gate": dense(ks[16], (L, D_MODEL, D_FF), D_MODEL),
        "ffn2_w_up": dense(ks[17], (L, D_MODEL, D_FF), D_MODEL),
        "ffn2_w_down": dense(ks[18], (L, D_FF, D_MODEL), D_FF),
    }


def reference(x, positions, ffn1_norm, ffn1_w_gate, ffn1_w_up, ffn1_w_down,
              mix_norm, w_in, a_q_norm, a_k_norm, b_gate_w2, b_gate_bias, b_out_norm,
              w_a_up, w_b_up, w_out, ffn2_norm, ffn2_w_gate, ffn2_w_up, ffn2_w_down):
    B, S, _ = x.shape
    split_points = np.cumsum(IN_SPLITS)[:-1].tolist()
    for l in range(DEPTH):
        x = x + 0.5 * swiglu(rms_norm(x, ffn1_norm[l]), ffn1_w_gate[l], ffn1_w_up[l], ffn1_w_down[l])
        h = rms_norm(x, mix_norm[l])
        proj = h @ w_in[l]
        aq, ak, av, bq, bk, bv, br, bz, ga, gb = jnp.split(proj, split_points, axis=-1)
        aq = partial_rope(rms_norm(aq.reshape(B, S, A_HEADS, A_HEAD_DIM), a_q_norm[l]), positions)
        ak = partial_rope(rms_norm(ak.reshape(B, S, A_HEADS, A_HEAD_DIM), a_k_norm[l]), positions)
        av = av.reshape(B, S, A_HEADS, A_HEAD_DIM)
        o_a = dilated_attention(aq, ak, av).reshape(B, S, A_WIDTH)
        log_a = jax.nn.log_sigmoid((bz @ b_gate_w2[l] + b_gate_bias[l]).astype(jnp.float32)) / GATE_NORMALIZER
        o_b = gla_chunked(bq.reshape(B, S, B_HEADS, B_KEY_DIM) * (B_KEY_DIM ** -0.5),
                          bk.reshape(B, S, B_HEADS, B_KEY_DIM),
                          bv.reshape(B, S, B_HEADS, B_VAL_DIM),
                          log_a.reshape(B, S, B_HEADS, B_KEY_DIM))
        o_b = rms_norm(o_b, b_out_norm[l]).reshape(B, S, B_VAL_WIDTH) * jax.nn.silu(br)
        y = jax.nn.sigmoid(ga) * (o_a @ w_a_up[l]) + jax.nn.sigmoid(gb) * (o_b @ w_b_up[l])
        x = x + y @ w_out[l]
        x = x + 0.5 * swiglu(rms_norm(x, ffn2_norm[l]), ffn2_w_gate[l], ffn2_w_up[l], ffn2_w_down[l])
    return x
```

```python
import os
from contextlib import ExitStack
import numpy as np
import concourse.bass as bass
import concourse.mybir as mybir
from concourse.bass_utils import run_bass_kernel_spmd

F32 = mybir.dt.float32
BF16 = mybir.dt.bfloat16
I32 = mybir.dt.int32
AF = mybir.ActivationFunctionType
ALU = mybir.AluOpType

NCORE = 8
SEQ = 16384
D = 2048
DFF = 5632
NT = SEQ // NCORE
KC = D // 128
T = 1024
NTT = NT // T
NH = T // 512
INW = 10256
EPS = 1e-6
TWO_PI = 6.283185307179586
CW1, CW2, CW3 = 6.28125, 0.0019350052, 3.019916e-07
PI = 3.141592653589793

DEBUG = os.environ.get("MK_DEBUG", "")
OPT = int(os.environ.get("MK_OPT", "3"))


class Sched:
    ENG = ("pe", "act", "dve", "pool", "sp")
    DMA_Q = ("sp", "act", "pool")
    NRING = 8

    def __init__(self, nc):
        self.nc = nc
        self.streams = {e: [] for e in self.ENG}
        self.nsem = 0
        self.eng_sem = {}
        self.eng_cnt = {}
        for e in ("pe", "act", "dve", "pool"):
            self.eng_sem[e] = self._new_sem()
            self.eng_cnt[e] = 0
        self.ring = {q: [self._new_sem() for _ in range(self.NRING)] for q in self.DMA_Q}
        self.ring_cnt = {q: [0] * self.NRING for q in self.DMA_Q}
        self.ring_pos = {q: 0 for q in self.DMA_Q}
        self.waited = {e: {} for e in self.ENG}
        self.last_w = {}
        self.readers = {}
        self.nops = 0
        self.cc_toks = []

    def _new_sem(self):
        i = self.nsem
        self.nsem += 1
        return i

    def _need(self, eng, reads, writes, skip_sem=None):
        need = {}

        def add(tok):
            if tok is None:
                return
            s, v = tok
            if need.get(s, 0) < v:
                need[s] = v

        for r in reads:
            add(self.last_w.get(r))
        for w in writes:
            add(self.last_w.get(w))
            for t in self.readers.get(w, ()):
                add(t)
        out = []
        for s, v in need.items():
            if skip_sem is not None and s == skip_sem:
                continue
            if self.waited[eng].get(s, 0) >= v:
                continue
            self.waited[eng][s] = v
            out.append((s, v))
        return out

    def _commit(self, tok, reads, writes):
        for r in reads:
            lst = self.readers.setdefault(r, [])
            lst.append(tok)
            if len(lst) > 24:
                best = {}
                for s, v in lst:
                    if best.get(s, 0) < v:
                        best[s] = v
                self.readers[r] = list(best.items())
        for w in writes:
            self.last_w[w] = tok
            self.readers[w] = []

    def op(self, eng, fn, reads=(), writes=()):
        reads = list(reads)
        writes = list(writes)
        sem = self.eng_sem[eng]
        waits = self._need(eng, reads, writes, skip_sem=(sem if eng == "pe" else None))
        self.eng_cnt[eng] += 1
        tok = (sem, self.eng_cnt[eng])
        self.streams[eng].append(("op", waits, fn, sem, 1))
        self._commit(tok, reads, writes)
        self.nops += 1
        return tok

    def dma(self, q, fn, reads=(), writes=(), inc=16):
        reads = list(reads)
        writes = list(writes)
        pos = self.ring_pos[q]
        self.ring_pos[q] = (pos + 1) % self.NRING
        sem = self.ring[q][pos]
        waits = self._need(q, reads, writes)
        prev = self.ring_cnt[q][pos]
        if prev > 0 and self.waited[q].get(sem, 0) < prev:
            self.waited[q][sem] = prev
            waits.append((sem, prev))
        self.ring_cnt[q][pos] = prev + inc
        tok = (sem, prev + inc)
        self.streams[q].append(("op", waits, fn, sem, inc))
        self._commit(tok, reads, writes)
        self.nops += 1
        return tok

    def cc(self, fn, reads=(), writes=()):
        reads = list(reads)
        writes = list(writes)
        sem = self._new_sem()
        waits = self._need("pool", reads, writes)
        tok = (sem, 1)
        self.streams["pool"].append(("op", waits, fn, sem, 1))
        self._commit(tok, reads, writes)
        self.cc_toks.append(tok)
        return tok

    def wait_all_dma(self):
        for q in self.DMA_Q:
            waits = []
            for pos in range(self.NRING):
                v = self.ring_cnt[q][pos]
                sem = self.ring[q][pos]
                if v > 0 and self.waited[q].get(sem, 0) < v:
                    self.waited[q][sem] = v
                    waits.append((sem, v))
            if waits:
                self.streams[q].append(("wait", waits))

    def barrier_all(self):
        toks = []
        for e in ("pe", "act", "dve", "pool"):
            if self.eng_cnt[e] > 0:
                toks.append((self.eng_sem[e], self.eng_cnt[e]))
        for q in self.DMA_Q:
            for pos in range(self.NRING):
                v = self.ring_cnt[q][pos]
                if v > 0:
                    toks.append((self.ring[q][pos], v))
        toks.extend(self.cc_toks)
        for e in self.ENG:
            waits = []
            for s, v in toks:
                if e in self.eng_sem and s == self.eng_sem[e]:
                    continue
                if self.waited[e].get(s, 0) < v:
                    self.waited[e][s] = v
                    waits.append((s, v))
            if waits:
                self.streams[e].append(("wait", waits))
        self.last_w = {}
        self.readers = {}

    def emit(self, stack):
        nc = self.nc
        self.wait_all_dma()
        sems = [stack.enter_context(nc.semaphore(f"s{i}")) for i in range(self.nsem)]
        streams = self.streams

        def run(engobj, items):
            for it in items:
                if it[0] == "wait":
                    for s, v in it[1]:
                        engobj.wait_ge(sems[s], v)
                else:
                    _, waits, fn, sem, inc = it
                    for s, v in waits:
                        engobj.wait_ge(sems[s], v)
                    ins = fn(engobj)
                    ins.then_inc(sems[sem], inc)

        with nc.Block() as block:
            @block.tensor
            def _(e):
                run(e, streams["pe"])

            @block.scalar
            def _(e):
                run(e, streams["act"])

            @block.vector
            def _(e):
                run(e, streams["dve"])

            @block.gpsimd
            def _(e):
                run(e, streams["pool"])

            @block.sync
            def _(e):
                run(e, streams["sp"])


class Ctx:
    pass


class SubRing:
    def __init__(self, ring, idxs):
        self.items = [(ring.tiles[i], (ring.name, i)) for i in idxs]
        self.i = 0

    def next(self):
        it = self.items[self.i]
        self.i = (self.i + 1) % len(self.items)
        return it


def mm(S, out_ap, pairs, reads, writes):
    pairs = list(pairs)

    def fn(e):
        last = None
        n = len(pairs)
        for i, (l, r) in enumerate(pairs):
            last = e.matmul(out_ap, lhsT=l, rhs=r, start=(i == 0), stop=(i == n - 1))
        return last
    return S.op("pe", fn, reads, writes)


class Ring:
    def __init__(self, st, nc, name, n, shape, dtype, psum=False):
        alloc = nc.psum_tensor if psum else nc.sbuf_tensor
        self.tiles = [st.enter_context(alloc(f"r_{name}{i}", shape, dtype)) for i in range(n)]
        self.name = name
        self.n = n
        self.i = 0

    def next(self):
        i = self.i
        self.i = (i + 1) % self.n
        return self.tiles[i], (self.name, i)


def rms_feature_major(S, C, gcol, nchunks, src, src_key, dst, dst_key, ncols, n0, dst_n0=None):
    cols = slice(n0, n0 + ncols)
    dcols = cols if dst_n0 is None else slice(dst_n0, dst_n0 + ncols)
    pss, pk = C.ps.next()
    for k2 in range(0, nchunks, 2):
        sq, sk = C.sqring.next()
        S.op("act", lambda e, sq=sq, k2=k2: e.activation(out=sq[:, :, 0:ncols], in_=src[:, k2:k2 + 2, cols], func=AF.Square),
             reads=[src_key(k2), src_key(k2 + 1)], writes=[sk])

        def fn(e, sq=sq, k2=k2):
            last = None
            for j in range(2):
                last = e.matmul(pss[:, 0:ncols], lhsT=C.ones_bf[:], rhs=sq[:, j, 0:ncols],
                                start=(k2 == 0 and j == 0), stop=(k2 + j == nchunks - 1))
            return last
        S.op("pe", fn, reads=[sk], writes=[pk])
    rt, rk = C.f32ring.next()
    S.op("act", lambda e: e.activation(out=rt[:, 0:ncols], in_=pss[:, 0:ncols], func=AF.Ln,
                                       scale=1.0 / (128 * nchunks), bias=C.eps_col[:, 0:1]),
         reads=[pk], writes=[rk])
    S.op("act", lambda e: e.activation(out=rt[:, 0:ncols], in_=rt[:, 0:ncols], func=AF.Exp, scale=-0.5), reads=[rk], writes=[rk])
    for k in range(nchunks):
        S.op("dve", lambda e, k=k: e.scalar_tensor_tensor(out=dst[:, k, dcols], in0=src[:, k, cols], scalar=gcol[:, k:k + 1],
                                                        in1=rt[:, 0:ncols], op0=ALU.mult, op1=ALU.mult),
             reads=[src_key(k), rk], writes=[dst_key(k)])


def ffn_block(S, C, E, wg_d, wu_d, wd_d):
    xT, xn, hbuf, wload, xkey, xnkey = E.xT, E.xn, E.hbuf, E.wload, E.xkey, E.xnkey
    wgv = wg_d.rearrange("(k p) f -> p k f", p=128)
    wuv = wu_d.rearrange("(k p) f -> p k f", p=128)
    wdv = wd_d.rearrange("(j p) d -> p j d", p=128)
    NG = DFF // 256

    def gu(g):
        wg, wgk = wload(wgv[:, :, g * 256:(g + 1) * 256], KC, 256)
        wu, wuk = wload(wuv[:, :, g * 256:(g + 1) * 256], KC, 256)
        hb = hbuf[g % 2]
        for j in range(2):
            for n in range(NH):
                cs = slice(n * 512, (n + 1) * 512)
                pg, pgk = C.ps.next()
                pu, puk = C.ps.next()
                mm(S, pg[:], [(wg[:, k, j * 128:(j + 1) * 128], xn[:, k, cs]) for k in range(KC)],
                   [wgk] + [xnkey(k) for k in range(KC)], [pgk])
                mm(S, pu[:], [(wu[:, k, j * 128:(j + 1) * 128], xn[:, k, cs]) for k in range(KC)],
                   [wuk] + [xnkey(k) for k in range(KC)], [puk])
                sg, sgk = C.f32ring.next()
                S.op("act", lambda e, sg=sg, pg=pg: e.activation(out=sg[:], in_=pg[:], func=AF.Silu), reads=[pgk], writes=[sgk])
                S.op("dve", lambda e, sg=sg, pu=pu, hb=hb, j=j, cs=cs: e.tensor_tensor(out=hb[:, j, cs], in0=pu[:], in1=sg[:], op=ALU.mult),
                     reads=[puk, sgk], writes=[("h", g % 2, j, n)])

    def down(g):
        wd, wdk = wload(wdv[:, 2 * g:2 * g + 2, :], 2, D)
        hb = hbuf[g % 2]
        for i in range(KC):
            for n in range(NH):
                cs = slice(n * 512, (n + 1) * 512)
                py, pyk = C.ps.next()
                mm(S, py[:], [(wd[:, j, i * 128:(i + 1) * 128], hb[:, j, cs]) for j in range(2)],
                   [wdk] + [("h", g % 2, j, n) for j in range(2)], [pyk])
                S.op("dve", lambda e, py=py, i=i, cs=cs: e.scalar_tensor_tensor(out=xT[:, i, cs], in0=py[:], scalar=0.5, in1=xT[:, i, cs],
                                                                          op0=ALU.mult, op1=ALU.add),
                     reads=[pyk, xkey(i)], writes=[xkey(i)])

    gu(0)
    for g in range(NG):
        if g + 1 < NG:
            gu(g + 1)
        down(g)


def build_program(debug=""):
    nc = bass.Bass("TRN2", target_bir_lowering=False)
    dbg = set(debug.split(",")) if debug else set()
    dbg_set = dbg
    C = Ctx()
    S = Sched(nc)

    def din(name, shape, dt=F32):
        return nc.dram_tensor(name, list(shape), dt, kind="ExternalInput").ap()

    def dscratch(name, shape=None, dt=None, dbg=False, p1out=False, ccout=False):
        kind = "ExternalOutput" if (dbg and debug) else "Internal"
        if p1out and "skipp1" in dbg_set:
            kind = "ExternalInput"
        if ccout and "nocc" in dbg_set:
            kind = "ExternalInput"
        return nc.dram_tensor(name, list(shape), dt, kind=kind).ap()

    x = din("x", [NT, D])
    pos = din("pos", [1, NT], I32)
    w_g1 = din("ffn1_w_gate", [D, DFF]); w_u1 = din("ffn1_w_up", [D, DFF]); w_d1 = din("ffn1_w_down", [DFF, D])
    w_g2 = din("ffn2_w_gate", [D, DFF]); w_u2 = din("ffn2_w_up", [D, DFF]); w_d2 = din("ffn2_w_down", [DFF, D])
    w_in = din("w_in", [D, INW])
    w_aup = din("w_a_up", [1024, D]); w_bup = din("w_b_up", [1024, D]); w_out = din("w_out", [D, D])
    g_ffn1 = din("g_ffn1", [128, KC]); g_mix = din("g_mix", [128, KC]); g_ffn2 = din("g_ffn2", [128, KC])
    g_aq = din("g_aq", [128, 1]); g_ak = din("g_ak", [128, 1])
    gate_w2 = din("gate_w2", [16, 512]); gate_b = din("gate_b", [128, 4]); g_bout = din("g_bout", [128, 2])
    ident_in = din("ident", [128, 128]); ropeP_in = din("ropeP", [128, 128])
    inv_in = din("inv_col", [128, 1]); sgn_in = din("sgn_col", [128, 1])
    mprev_in = din("mprev", [128, 128]); mcur_in = din("mcur", [128, 128])
    onehot_in = din("onehot", [128, NCORE]); hasprev_in = din("hasprev", [128, 1]); gmask_in = din("gmask", [128, NCORE])

    out = nc.dram_tensor("out", [NT, D], F32, kind="ExternalOutput").ap()

    x1s = dscratch("x1s", shape=[128, KC, NT], dt=F32, dbg=True, p1out=True)
    aqs = dscratch("aqs", shape=[128, 8, NT], dt=BF16, dbg=True, p1out=True)
    kv = dscratch("kv", shape=[128, 2 * 8 * NT], dt=BF16, dbg=True, p1out=True)
    bqs = dscratch("bqs", shape=[128, 4, NT], dt=BF16, dbg=True, p1out=True)
    bks = dscratch("bks", shape=[128, 4, NT], dt=BF16, dbg=True, p1out=True)
    bvs = dscratch("bvs", shape=[NT // 128, 128, 1024], dt=BF16, dbg=True, p1out=True)
    brs = dscratch("brs", shape=[128, 8, NT], dt=BF16, dbg=True, p1out=True)
    bzs = dscratch("bzs", shape=[16, NT], dt=F32, dbg=True, p1out=True)
    gas = dscratch("gas", shape=[128, 16, NT], dt=BF16, dbg=True, p1out=True)
    gbs = dscratch("gbs", shape=[128, 16, NT], dt=BF16, dbg=True, p1out=True)

    kvg = dscratch("kvg", shape=[NCORE * 128, 2 * 8 * NT], dt=BF16, ccout=True)
    oas = dscratch("oas", shape=[128, 8, NT], dt=BF16, dbg=True)
    obs = dscratch("obs", shape=[128, 8, NT], dt=BF16, dbg=True)
    GW = 4 * 264
    gst = dscratch("gst", shape=[128, GW], dt=F32, dbg=True)
    gstg = dscratch("gstg", shape=[NCORE * 128, GW], dt=F32, ccout=True)
    x2s = dscratch("x2s", shape=[128, KC, NT], dt=F32, dbg=True)

    with ExitStack() as st:
        def sb(name, shape, dt):
            return st.enter_context(nc.sbuf_tensor("s_" + name, list(shape), dt))

        ident = sb("ident", [128, 128], F32)
        ident_bf = sb("ident_bf", [128, 128], BF16)
        ones_bf = sb("ones_bf", [128, 128], BF16)
        ropeP = sb("ropeP", [128, 128], F32)
        gcols = sb("gcols", [128, 3 * KC], F32)
        smallc = sb("smallc", [128, 16], F32)
        C.ident = ident; C.ident_bf = ident_bf; C.ones_bf = ones_bf
        S.dma("sp", lambda e: e.dma_start(out=ident[:], in_=ident_in[:, :]), writes=["ident"])
        S.dma("sp", lambda e: e.dma_start(out=ropeP[:], in_=ropeP_in[:, :]), writes=["ropeP"])
        for i, g in enumerate((g_ffn1, g_mix, g_ffn2)):
            S.dma("sp", lambda e, i=i, g=g: e.dma_start(out=gcols[:, i * KC:(i + 1) * KC], in_=g[:, :]), writes=["gcols"])
        for i, a in enumerate((g_aq, g_ak, inv_in, sgn_in)):
            S.dma("sp", lambda e, i=i, a=a: e.dma_start(out=smallc[:, i:i + 1], in_=a[:, :]), writes=["smallc"])
        S.dma("sp", lambda e: e.dma_start(out=smallc[:, 5:6], in_=hasprev_in[:, :]), writes=["smallc"])
        S.dma("sp", lambda e: e.dma_start(out=smallc[:, 6:8], in_=g_bout[:, :]), writes=["smallc"])
        S.dma("sp", lambda e: e.dma_start(out=smallc[:, 8:12], in_=gate_b[:, :]), writes=["smallc"])
        S.op("dve", lambda e: e.memset(smallc[:, 4:5], EPS), writes=["smallc"], reads=["smallc"])
        S.op("dve", lambda e: e.memset(smallc[:, 12:13], 1.0), writes=["smallc"], reads=["smallc"])
        S.op("dve", lambda e: e.tensor_copy(out=ident_bf[:], in_=ident[:]), reads=["ident"], writes=["ident_bf"])
        S.op("dve", lambda e: e.memset(ones_bf[:], 1.0), writes=["ones_bf"])
        C.eps_col = smallc[:, 4:5]
        S.barrier_all()

        with ExitStack() as p1:
          if "skipp1" not in dbg:
            def sb1(name, shape, dt):
                return p1.enter_context(nc.sbuf_tensor("s_" + name, list(shape), dt))

            xT = sb1("xT", [128, KC, T], F32)
            xn = sb1("xn", [128, KC, T], BF16)
            C.ps = Ring(p1, nc, "ps", 8, [128, 512], F32, psum=True)
            C.sqring = Ring(p1, nc, "sq", 3, [128, 2, 512], BF16)
            C.f32ring = Ring(p1, nc, "f32r", 4, [128, 512], F32)
            wring = Ring(p1, nc, "w", 6, [128, 4096], BF16)
            hbuf = [sb1(f"h{i}", [128, 2, T], BF16) for i in range(2)]
            stage = Ring(p1, nc, "stage", 2, [128, D], F32)
            obring = Ring(p1, nc, "ob", 4, [128, T], BF16)
            cosT = sb1("cosT", [128, T], F32)
            sinT = sb1("sinT", [128, T], F32)
            posi = sb1("posi", [128, T], I32)

            def wload(view, a, b):
                wt, wk = wring.next()
                dst = wt[:, 0:a * b].rearrange("p (a b) -> p a b", a=a)
                S.dma("pool", lambda e: e.dma_start(out=dst, in_=view), writes=[wk])
                return dst, wk

            xkey = lambda k: ("x", k)
            xnkey = lambda k: ("xn", k)
            E1 = Ctx()
            E1.xT, E1.xn, E1.hbuf, E1.wload, E1.xkey, E1.xnkey = xT, xn, hbuf, wload, xkey, xnkey

            def tile_body(tt):
                tok0 = tt * T
                for sub in range(T // 128):
                    stg, sk = stage.next()
                    S.dma("sp", lambda e, stg=stg, sub=sub: e.dma_start(out=stg[:], in_=x[tok0 + sub * 128: tok0 + (sub + 1) * 128, :]),
                          writes=[sk])
                    for k4 in range(0, KC, 4):
                        pt, pk = C.ps.next()

                        def tr(e, stg=stg, k4=k4, pt=pt):
                            last = None
                            for j in range(4):
                                last = e.transpose(pt[:, j * 128:(j + 1) * 128], stg[:, (k4 + j) * 128:(k4 + j + 1) * 128], ident[:])
                            return last
                        S.op("pe", tr, reads=[sk], writes=[pk])
                        eng = "act" if (k4 // 4) % 2 == 0 else "dve"
                        if eng == "act":
                            S.op("act", lambda e, pt=pt, k4=k4, sub=sub: e.activation(
                                out=xT[:, k4:k4 + 4, sub * 128:(sub + 1) * 128], in_=pt[:].rearrange("p (j t) -> p j t", j=4), func=AF.Copy),
                                reads=[pk], writes=[xkey(k4 + j) for j in range(4)])
                        else:
                            S.op("dve", lambda e, pt=pt, k4=k4, sub=sub: e.tensor_copy(
                                out=xT[:, k4:k4 + 4, sub * 128:(sub + 1) * 128], in_=pt[:].rearrange("p (j t) -> p j t", j=4)),
                                reads=[pk], writes=[xkey(k4 + j) for j in range(4)])
                if "norope" not in dbg:
                    rope_tables(tok0)

                for n in range(NH):
                    rms_feature_major(S, C, gcols[:, 0:KC], KC, xT, xkey, xn, xnkey, 512, n * 512)
                rest_of_tile(tt, tok0)

            def rope_tables(tok0):
                S.dma("sp", lambda e: e.dma_start(out=posi[:], in_=pos[0:1, tok0:tok0 + T].partition_broadcast(128)), writes=["posi"])
                ang = cosT
                S.op("dve", lambda e: e.tensor_copy(out=sinT[:], in_=posi[:]), reads=["posi"], writes=["sinT"])
                S.op("dve", lambda e: e.tensor_scalar(out=ang[:], in0=sinT[:], scalar1=smallc[:, 2:3], scalar2=None, op0=ALU.mult),
                     reads=["sinT"], writes=["cosT"])
                S.op("dve", lambda e: e.tensor_scalar(out=sinT[:], in0=ang[:], scalar1=1.0 / TWO_PI, scalar2=None, op0=ALU.mult),
                     reads=["cosT"], writes=["sinT"])
                S.op("dve", lambda e: e.tensor_copy(out=posi[:], in_=sinT[:]), reads=["sinT"], writes=["posi"])
                S.op("dve", lambda e: e.tensor_copy(out=sinT[:], in_=posi[:]), reads=["posi"], writes=["sinT"])
                for cw in (CW1, CW2, CW3):
                    S.op("dve", lambda e, cw=cw: e.scalar_tensor_tensor(out=ang[:], in0=sinT[:], scalar=-cw, in1=ang[:], op0=ALU.mult, op1=ALU.add),
                         reads=["sinT", "cosT"], writes=["cosT"])
                S.op("dve", lambda e: e.tensor_scalar(out=sinT[:], in0=ang[:], scalar1=PI, scalar2=-TWO_PI, op0=ALU.is_gt, op1=ALU.mult),
                     reads=["cosT"], writes=["sinT"])
                S.op("dve", lambda e: e.tensor_tensor(out=ang[:], in0=ang[:], in1=sinT[:], op=ALU.add), reads=["cosT", "sinT"], writes=["cosT"])
                S.op("dve", lambda e: e.tensor_scalar(out=sinT[:], in0=ang[:], scalar1=-PI, scalar2=TWO_PI, op0=ALU.is_lt, op1=ALU.mult),
                     reads=["cosT"], writes=["sinT"])
                S.op("dve", lambda e: e.tensor_tensor(out=ang[:], in0=ang[:], in1=sinT[:], op=ALU.add), reads=["cosT", "sinT"], writes=["cosT"])
                S.op("dve", lambda e: e.tensor_scalar(out=ang[:], in0=ang[:], scalar1=PI, scalar2=-PI, op0=ALU.min, op1=ALU.max),
                     reads=["cosT"], writes=["cosT"])
                S.op("act", lambda e: e.activation(out=sinT[:], in_=ang[:], func=AF.Sin, scale=smallc[:, 3:4]), reads=["cosT"], writes=["sinT"])
                S.op("dve", lambda e: e.tensor_scalar(out=ang[:], in0=ang[:], scalar1=PI / 2, scalar2=None, op0=ALU.add),
                     reads=["cosT", "sinT"], writes=["cosT"])
                tmpc, tck = C.f32ring.next()
                for hh in range(NH):
                    cs = slice(hh * 512, (hh + 1) * 512)
                    S.op("dve", lambda e, cs=cs: e.tensor_scalar(out=tmpc[:], in0=ang[:, cs], scalar1=PI, scalar2=-TWO_PI, op0=ALU.is_gt, op1=ALU.mult),
                         reads=["cosT"], writes=[tck])
                    S.op("dve", lambda e, cs=cs: e.tensor_tensor(out=ang[:, cs], in0=ang[:, cs], in1=tmpc[:], op=ALU.add), reads=["cosT", tck], writes=["cosT"])
                S.op("dve", lambda e: e.tensor_scalar(out=ang[:], in0=ang[:], scalar1=PI, scalar2=-PI, op0=ALU.min, op1=ALU.max),
                     reads=["cosT"], writes=["cosT"])
                S.op("act", lambda e: e.activation(out=cosT[:], in_=ang[:], func=AF.Sin), reads=["cosT"], writes=["cosT"])

            def rest_of_tile(tt, tok0):
                ffn_block(S, C, E1, w_g1, w_u1, w_d1)

                for k in range(KC):
                    S.dma("sp", lambda e, k=k: e.dma_start(out=x1s[:, k, tok0:tok0 + T], in_=xT[:, k, :]), reads=[xkey(k)], writes=[("x1s", k, tt)])

                if "ffn1" in dbg or "ffn1s" in dbg:
                    return
                for n in range(NH):
                    rms_feature_major(S, C, gcols[:, KC:2 * KC], KC, xT, xkey, xn, xnkey, 512, n * 512)

                w_inv = w_in.rearrange("(k p) f -> p k f", p=128)

                pend = []

                def advance(flush=False):
                    while True:
                        for ent in list(pend):
                            ent.pop(0)()
                            if not ent:
                                pend.remove(ent)
                        if not flush or not pend:
                            break

                def proj_chunk(col0, ncol, epilogue):
                    w, wk = wload(w_inv[:, :, col0:col0 + ncol], KC, ncol)
                    for j in range((ncol + 127) // 128):
                        m = min(128, ncol - j * 128)
                        for n in range(NH):
                            cs = slice(n * 512, (n + 1) * 512)
                            pt, pk = C.ps.next()
                            mm(S, pt[0:m, :], [(w[:, k, j * 128:j * 128 + m], xn[:, k, cs]) for k in range(KC)],
                               [wk] + [xnkey(k) for k in range(KC)], [pk])
                            advance()
                            st_ = epilogue(pt, pk, j, n)
                            pend.append(list(st_))

                def store_bf(dst_fn):
                    state = {}

                    def ep(pt, pk, j, n, func=None):
                        raise NotImplementedError
                    return ep

                def qk_epilogue(which, head):
                    gc = smallc[:, 0:1] if which == "q" else smallc[:, 1:2]
                    ot_box = {}

                    def ep(pt, pk, j, n):
                        cs = slice(n * 512, (n + 1) * 512)
                        box = {}

                        def stage1():
                            if n == 0:
                                ot_box["t"] = obring.next()
                            box["ot"] = ot_box["t"]
                            sq, sk = C.sqring.next()
                            S.op("act", lambda e: e.activation(out=sq[:, 0, :], in_=pt[:], func=AF.Square), reads=[pk], writes=[sk])
                            pss, psk = C.ps.next()
                            mm(S, pss[:], [(ones_bf[:], sq[:, 0, :])], [sk], [psk])
                            box["pss"] = (pss, psk)

                        def stage2():
                            pss, psk = box["pss"]
                            rt, rk = C.f32ring.next()
                            S.op("act", lambda e: e.activation(out=rt[:], in_=pss[:], func=AF.Ln, scale=1.0 / 128, bias=C.eps_col), reads=[psk], writes=[rk])
                            S.op("act", lambda e: e.activation(out=rt[:], in_=rt[:], func=AF.Exp, scale=-0.5), reads=[rk], writes=[rk])
                            qn, qnk = C.f32ring.next()
                            S.op("dve", lambda e: e.scalar_tensor_tensor(out=qn[:], in0=pt[:], scalar=gc, in1=rt[:], op0=ALU.mult, op1=ALU.mult),
                                 reads=[pk, rk], writes=[qnk])
                            pr, prk = C.ps.next()
                            mm(S, pr[:], [(ropeP[:], qn[:])], [qnk], [prk])
                            box["r"] = (rt, rk, qn, qnk, pr, prk)

                        def stage3():
                            rt, rk, qn, qnk, pr, prk = box["r"]
                            ot, ok = box["ot"]
                            S.op("dve", lambda e: e.tensor_tensor(out=rt[:], in0=pr[:], in1=sinT[:, cs], op=ALU.mult), reads=[prk, "sinT", rk], writes=[rk])
                            S.op("pool", lambda e: e.tensor_tensor(out=qn[:], in0=qn[:], in1=cosT[:, cs], op=ALU.mult), reads=[qnk, "cosT"], writes=[qnk])
                            S.op("dve", lambda e: e.tensor_tensor(out=ot[:, cs], in0=qn[:], in1=rt[:], op=ALU.add), reads=[qnk, rk], writes=[ok])
                            if n == NH - 1:
                                if which == "q":
                                    dst = aqs[:, head, tok0:tok0 + T]
                                else:
                                    dst = kv[:, head * NT + tok0: head * NT + tok0 + T]
                                S.dma("sp", lambda e: e.dma_start(out=dst, in_=ot[:]), reads=[ok], writes=[("dr", which, head, tt)])
                        return [stage1, stage2, stage3]
                    return ep

                def plain_epilogue(dst_fn, func=None, scale=1.0):
                    ot_box = {}

                    def ep(pt, pk, j, n):
                      def stage():
                        cs = slice(n * 512, (n + 1) * 512)
                        if n == 0:
                            ot_box["t"] = obring.next()
                        ot, ok = ot_box["t"]
                        if func is None:
                            if scale == 1.0 and (j + n) % 2 == 0:
                                S.op("dve", lambda e: e.tensor_copy(out=ot[:, cs], in_=pt[:]), reads=[pk], writes=[ok])
                            else:
                                S.op("act", lambda e: e.activation(out=ot[:, cs], in_=pt[:], func=AF.Copy, scale=scale), reads=[pk], writes=[ok])
                        else:
                            S.op("act", lambda e: e.activation(out=ot[:, cs], in_=pt[:], func=func), reads=[pk], writes=[ok])
                        if n == NH - 1:
                            dst = dst_fn(j)
                            S.dma("sp", lambda e: e.dma_start(out=dst, in_=ot[:]), reads=[ok], writes=[("dr2", id(dst_fn), j, tt)])
                      return [stage]
                    return ep

                for hp in range(0, 8, 2):
                    eps_ = [qk_epilogue("q", hp), qk_epilogue("q", hp + 1)]
                    proj_chunk(hp * 128, 256, lambda pt, pk, j, n: eps_[j](pt, pk, j, n))
                for hp in range(0, 8, 2):
                    eps_ = [qk_epilogue("k", hp), qk_epilogue("k", hp + 1)]
                    proj_chunk(1024 + hp * 128, 256, lambda pt, pk, j, n: eps_[j](pt, pk, j, n))
                for hp in range(0, 8, 2):
                    proj_chunk(2048 + hp * 128, 256, plain_epilogue(
                        lambda j, hp=hp: kv[:, 8 * NT + (hp + j) * NT + tok0: 8 * NT + (hp + j) * NT + tok0 + T]))
                for c2 in range(0, 4, 2):
                    proj_chunk(3072 + c2 * 128, 256, plain_epilogue(lambda j, c2=c2: bqs[:, c2 + j, tok0:tok0 + T], scale=128 ** -0.5))
                for c2 in range(0, 4, 2):
                    proj_chunk(3584 + c2 * 128, 256, plain_epilogue(lambda j, c2=c2: bks[:, c2 + j, tok0:tok0 + T]))
                advance(flush=True)
                for c2 in range(0, 8, 2):
                    w, wk = wload(w_inv[:, :, 4096 + c2 * 128: 4096 + c2 * 128 + 256], KC, 256)
                    for sub in range(T // 128):
                        pt, pk = C.ps.next()
                        mm(S, pt[:, 0:256], [(xn[:, k, sub * 128:(sub + 1) * 128], w[:, k, :]) for k in range(KC)],
                           [wk] + [xnkey(k) for k in range(KC)], [pk])
                        ot, ok = obring.next()
                        if sub % 2 == 0:
                            S.op("dve", lambda e, ot=ot, pt=pt: e.tensor_copy(out=ot[:, 0:256], in_=pt[:, 0:256]), reads=[pk], writes=[ok])
                        else:
                            S.op("act", lambda e, ot=ot, pt=pt: e.activation(out=ot[:, 0:256], in_=pt[:, 0:256], func=AF.Copy), reads=[pk], writes=[ok])
                        S.dma("sp", lambda e, ot=ot, sub=sub, c2=c2: e.dma_start(out=bvs[tt * (T // 128) + sub, :, c2 * 128:c2 * 128 + 256], in_=ot[:, 0:256]),
                              reads=[ok], writes=[("bvs", tt, sub, c2)])
                for c2 in range(0, 8, 2):
                    proj_chunk(5120 + c2 * 128, 256, plain_epilogue(lambda j, c2=c2: brs[:, c2 + j, tok0:tok0 + T], func=AF.Silu))
                def bz_ep(pt, pk, j, n):
                    def stage():
                        zt, zk = C.f32ring.next()
                        S.op("dve", lambda e: e.tensor_copy(out=zt[0:16, :], in_=pt[0:16, :]), reads=[pk], writes=[zk])
                        S.dma("sp", lambda e: e.dma_start(out=bzs[:, tok0 + n * 512: tok0 + (n + 1) * 512], in_=zt[0:16, :]), reads=[zk], writes=[("bzs", tt, n)])
                    return [stage]
                proj_chunk(6144, 16, bz_ep)
                for c2 in range(0, 16, 2):
                    proj_chunk(6160 + c2 * 128, 256, plain_epilogue(lambda j, c2=c2: gas[:, c2 + j, tok0:tok0 + T], func=AF.Sigmoid))
                for c2 in range(0, 16, 2):
                    proj_chunk(8208 + c2 * 128, 256, plain_epilogue(lambda j, c2=c2: gbs[:, c2 + j, tok0:tok0 + T], func=AF.Sigmoid))
                advance(flush=True)

            for tt in range(NTT if "ffn1s" not in dbg else 1):
                tile_body(tt)
            S.barrier_all()

        rg = [list(range(NCORE))]
        if "nocc" not in dbg and "p1only" not in dbg:
            S.cc(lambda e: e.collective_compute("AllGather", ALU.bypass, replica_groups=rg,
                                                ins=[kv.opt()], outs=[kvg.opt()], dma_qos="P2"),
                 reads=[], writes=["kvg"])
        if "p1only" not in dbg:
          with ExitStack() as p3:
            def sb3(name, shape, dt):
                return p3.enter_context(nc.sbuf_tensor("s3_" + name, list(shape), dt))
            ps3 = Ring(p3, nc, "ps3", 6, [128, 512], F32, psum=True)
            pb3 = Ring(p3, nc, "pb3", 2, [128, 1024], BF16, psum=True)
            C.ps = ps3
            oloc = sb3("oloc", [128, 8, NT], F32)
            qE = sb3("qE", [128, 4, NT], BF16)
            gstt = sb3("gstt", [128, GW], F32)
            maskU = sb3("maskU", [128, 128], F32)
            mprev = sb3("mprevf", [128, 128], F32)
            onesf = sb3("onesf", [128, 128], F32)
            ohc = sb3("ohc", [128, NCORE], F32)
            gmc = sb3("gmc", [128, NCORE], F32)
            S.dma("sp", lambda e: e.dma_start(out=maskU[:], in_=mcur_in[:, :]), writes=["maskU"])
            S.dma("sp", lambda e: e.dma_start(out=mprev[:], in_=mprev_in[:, :]), writes=["mprev"])
            S.dma("sp", lambda e: e.dma_start(out=ohc[:], in_=onehot_in[:, :]), writes=["ohc"])
            S.dma("sp", lambda e: e.dma_start(out=gmc[:], in_=gmask_in[:, :]), writes=["gmc"])
            S.op("pool", lambda e: e.memset(onesf[:], 1.0), writes=["onesf"])

            with ExitStack() as pg:
                def sbg(name, shape, dt):
                    return pg.enter_context(nc.sbuf_tensor("sg_" + name, list(shape), dt))
                w2sb = sbg("w2sb", [16, 512], F32)
                bzT = sbg("bzT", [16, NT], F32)
                negb = sbg("negb", [128, 4], F32)
                S.dma("sp", lambda e: e.dma_start(out=w2sb[:], in_=gate_w2[:, :]), writes=["w2sb"])
                S.dma("sp", lambda e: e.dma_start(out=bzT[:], in_=bzs[:, :]), writes=["bzT"])
                S.op("dve", lambda e: e.tensor_scalar(out=negb[:], in0=smallc[:, 8:12], scalar1=-1.0, scalar2=None, op0=ALU.mult), writes=["negb"])
                vtok = sbg("vtok", [128, NT // 128, 1024], BF16)
                for n_ in range(NT // 128):
                    S.dma("sp", lambda e, n_=n_: e.dma_start(out=vtok[:, n_, :], in_=bvs[n_, :, :]), writes=[("vtok", n_)])
                qh = [sbg(f"qh{i}", [128, NT], BF16) for i in range(2)]
                kh = [sbg(f"kh{i}", [128, NT], BF16) for i in range(2)]
                lsp = [sbg(f"lsp{i}", [128, NT], F32) for i in range(2)]
                csb = [sbg(f"cs{i}", [128, NT], F32) for i in range(2)]
                Sst = [sbg(f"S{i}", [128, 256], F32) for i in range(2)]
                Sbf = [sbg(f"Sbf{i}", [128, 256], BF16) for i in range(2)]
                Bp = [sbg(f"Bp{i}", [128, NT // 128 + 1], F32) for i in range(2)]
                e32 = Ring(pg, nc, "ge32", 14, [128, 128], F32)
                b16 = Ring(pg, nc, "gb16", 22, [128, 128], BF16)

                def gla_bufs(h):
                    hb = h % 2
                    return qh[hb], kh[hb], lsp[hb], csb[hb], Sst[hb], Sbf[hb], Bp[hb], (lambda nm: ("gla", nm, hb))

                def gla_head(h):
                    q_, k_, l_, c_, S_, Sb_, B_, K_ = gla_bufs(h)
                    S.dma("sp", lambda e: e.dma_start(out=q_[:], in_=bqs[:, h, :]), writes=[K_("q")])
                    S.dma("sp", lambda e: e.dma_start(out=k_[:], in_=bks[:, h, :]), writes=[K_("k")])
                    return q_, k_, l_, c_, S_, Sb_, B_, K_

                def gla_setup(h):
                    q_, k_, l_, c_, S_, Sb_, B_, K_ = gla_head(h)
                    for tq in range(NT // 512):
                        cs4 = slice(tq * 512, (tq + 1) * 512)
                        pz, pzk = ps3.next()
                        mm(S, pz[:], [(w2sb[0:16, h * 128:(h + 1) * 128], bzT[0:16, cs4])], ["w2sb", "bzT"], [pzk])
                        S.op("act", lambda e, pz=pz, cs4=cs4: e.activation(out=l_[:, cs4], in_=pz[:], func=AF.Exp, scale=-1.0, bias=negb[:, h:h + 1]),
                             reads=[pzk, "negb"], writes=[K_("l")])
                        S.op("act", lambda e, cs4=cs4: e.activation(out=l_[:, cs4], in_=l_[:, cs4], func=AF.Ln, bias=smallc[:, 12:13]),
                             reads=[K_("l")], writes=[K_("l")])
                    S.op("pool", lambda e: e.memset(B_[:, 0:1], 0.0), writes=[K_("B")])

                gctx = {}
                psA = SubRing(ps3, [0, 1])
                psU = SubRing(ps3, [2, 3, 4, 5])

                def gla_prep(h, n):
                    q_, k_, l_, c_, S_, Sb_, B_, K_ = gla_bufs(h)
                    cs = slice(n * 128, (n + 1) * 128)
                    S.op("dve", lambda e: e.tensor_tensor_scan(out=c_[:, cs], data0=onesf[:], data1=l_[:, cs], initial=0.0, op0=ALU.mult, op1=ALU.add),
                         reads=[K_("l"), "onesf"], writes=[K_("c")])
                    E0, E0k = e32.next()
                    E1, E1k = e32.next()
                    E3, E3k = e32.next()
                    S.op("act", lambda e: e.activation(out=E0[:], in_=c_[:, cs], func=AF.Exp, scale=-1.0 / 16), reads=[K_("c")], writes=[E0k])
                    S.op("act", lambda e: e.activation(out=E1[:], in_=c_[:, cs], func=AF.Exp, scale=1.0 / 16), reads=[K_("c")], writes=[E1k])
                    S.op("act", lambda e: e.activation(out=E3[:], in_=c_[:, cs], func=AF.Exp, scale=-1.0 / 16, bias=B_[:, n:n + 1]),
                         reads=[K_("c"), K_("B")], writes=[E3k])
                    S.op("dve", lambda e: e.scalar_tensor_tensor(out=B_[:, n + 1:n + 2], in0=c_[:, n * 128 + 127:n * 128 + 128], scalar=-1.0 / 16,
                                                                 in1=B_[:, n:n + 1], op0=ALU.mult, op1=ALU.add),
                         reads=[K_("c"), K_("B")], writes=[K_("B")])
                    qe, qek = b16.next()
                    ki, kik = b16.next()
                    ke, kek = b16.next()
                    S.op("dve", lambda e: e.tensor_tensor(out=qe[:], in0=q_[:, cs], in1=E0[:], op=ALU.mult), reads=[K_("q"), E0k], writes=[qek])
                    S.op("pool", lambda e: e.tensor_tensor(out=ki[:], in0=k_[:, cs], in1=E1[:], op=ALU.mult), reads=[K_("k"), E1k], writes=[kik])
                    S.op("pool", lambda e: e.tensor_scalar(out=ke[:], in0=ki[:], scalar1=E0[:, 127:128], scalar2=None, op0=ALU.mult),
                         reads=[kik, E0k], writes=[kek])
                    S.op("pool", lambda e: e.tensor_tensor(out=qE[:, h, cs], in0=q_[:, cs], in1=E3[:], op=ALU.mult), reads=[K_("q"), E3k], writes=[("qE", h)])
                    ptb, ptbk = pb3.next()
                    S.op("pe", lambda e: e.transpose(ptb[:, 0:128], ke[:], ident_bf[:]), reads=[kek], writes=[ptbk])
                    ket, ketk = b16.next()
                    S.op("act", lambda e: e.activation(out=ket[:], in_=ptb[:, 0:128], func=AF.Copy), reads=[ptbk], writes=[ketk])
                    pA, pAk = psA.next()
                    mm(S, pA[:, 0:128], [(ki[:], qe[:])], [kik, qek], [pAk])
                    Am, Amk = b16.next()
                    S.op("dve", lambda e: e.tensor_tensor(out=Am[:], in0=pA[:, 0:128], in1=maskU[:], op=ALU.mult), reads=[pAk, "maskU"], writes=[Amk])
                    pu, puk = psU.next()
                    mm(S, pu[:, 0:256], [(ket[:], vtok[:, n, h * 256:(h + 1) * 256])], [ketk, ("vtok", n)], [puk])
                    gctx[(h, n)] = (E0, E0k, qe, qek, Am, Amk, pu, puk)

                def gla_rec(h, n):
                    q_, k_, l_, c_, S_, Sb_, B_, K_ = gla_bufs(h)
                    cs = slice(n * 128, (n + 1) * 128)
                    E0, E0k, qe, qek, Am, Amk, pu, puk = gctx.pop((h, n))
                    po, pok = psA.next()

                    def fo(e):
                        last = None
                        for ec in range(2):
                            last = e.matmul(po[:, ec * 128:(ec + 1) * 128], lhsT=vtok[:, n, h * 256 + ec * 128: h * 256 + (ec + 1) * 128], rhs=Am[:],
                                            start=True, stop=(n == 0))
                            if n > 0:
                                last = e.matmul(po[:, ec * 128:(ec + 1) * 128], lhsT=Sb_[:, ec * 128:(ec + 1) * 128], rhs=qe[:], start=False, stop=True)
                        return last
                    S.op("pe", fo, reads=[("vtok", n), Amk, qek, K_("Sb")], writes=[pok])
                    S.op("act", lambda e: e.activation(out=oloc[:, 2 * h:2 * h + 2, cs], in_=po[:, 0:256].rearrange("p (a t) -> p a t", a=2), func=AF.Copy),
                         reads=[pok], writes=[("oloc", h)])
                    if n == 0:
                        S.op("dve", lambda e: e.tensor_copy(out=S_[:], in_=pu[:, 0:256]), reads=[puk], writes=[K_("S")])
                    else:
                        S.op("dve", lambda e: e.scalar_tensor_tensor(out=S_[:], in0=S_[:], scalar=E0[:, 127:128], in1=pu[:, 0:256], op0=ALU.mult, op1=ALU.add),
                             reads=[puk, E0k, K_("S")], writes=[K_("S")])
                    S.op("pool", lambda e: e.tensor_copy(out=Sb_[:], in_=S_[:]), reads=[K_("S")], writes=[K_("Sb")])

                def gla_chunk(h, n):
                    gla_prep(h, n)
                    gla_rec(h, n)

                def gla_export(h):
                    q_, k_, l_, c_, S_, Sb_, B_, K_ = gla_bufs(h)
                    S.op("pool", lambda e: e.tensor_copy(out=gstt[:, h * 264:h * 264 + 256], in_=S_[:]), reads=[K_("S")], writes=["gstt"])
                    S.op("act", lambda e: e.activation(out=gstt[:, h * 264 + 256:h * 264 + 257], in_=B_[:, NT // 128:NT // 128 + 1], func=AF.Exp),
                         reads=[K_("B")], writes=["gstt"])

                S.op("pool", lambda e: e.memset(gstt[:], 0.0), writes=["gstt"])
                for hp in (0, 2):
                    if OPT & 1:
                        gla_setup(hp)
                        gla_setup(hp + 1)
                        NCH = NT // 128
                        gla_prep(hp, 0)
                        gla_prep(hp + 1, 0)
                        for n in range(NCH):
                            if n + 1 < NCH:
                                gla_prep(hp, n + 1)
                                gla_prep(hp + 1, n + 1)
                            gla_rec(hp, n)
                            gla_rec(hp + 1, n)
                        gla_export(hp)
                        gla_export(hp + 1)
                    else:
                        for h_ in (hp, hp + 1):
                            gla_setup(h_)
                            for n in range(NT // 128):
                                gla_chunk(h_, n)
                            gla_export(h_)
                S.dma("sp", lambda e: e.dma_start(out=gst[:, :], in_=gstt[:]), reads=["gstt"], writes=["gst"])
                S.barrier_all()
            if "nocc" not in dbg:
                S.cc(lambda e: e.collective_compute("AllGather", ALU.bypass, replica_groups=rg,
                                                    ins=[gst.opt()], outs=[gstg.opt()]),
                     reads=["gst"], writes=["gstg"])

            with ExitStack() as pa:
                def sba(name, shape, dt):
                    return pa.enter_context(nc.sbuf_tensor("sa_" + name, list(shape), dt))
                ohm = sba("ohm", [128, NCORE, 128], BF16)
                for r in range(NCORE):
                    S.op("dve", lambda e, r=r: e.tensor_scalar(out=ohm[:, r, :], in0=ident[:], scalar1=ohc[:, r:r + 1], scalar2=None, op0=ALU.mult),
                         reads=["ohc"], writes=["ohm"])
                mk = {}
                mpf = sba("mpf", [128, 128], F32)
                S.op("dve", lambda e: e.tensor_scalar(out=mpf[:], in0=mprev[:], scalar1=smallc[:, 5:6], scalar2=None, op0=ALU.mult), reads=["mprev"], writes=["mpf"])
                for name, firsts in (("ff", (1, 1)), ("fn", (1, 0)), ("nn", (0, 0))):
                    mt = sba("mk" + name, [128, 512], BF16)
                    for u in range(2):
                        src = mpf if firsts[u] else mprev
                        S.op("dve", lambda e, mt=mt, u=u, src=src: e.tensor_copy(out=mt[:, u * 128:u * 128 + 128], in_=src[:]), reads=["mpf", "mprev"], writes=[("mk", name)])
                        S.op("dve", lambda e, mt=mt, u=u: e.tensor_copy(out=mt[:, 256 + u * 128:256 + u * 128 + 128], in_=maskU[:]), reads=["maskU"], writes=[("mk", name)])
                    mk[name] = mt
                qa = [sba(f"qa{i}", [128, NT], BF16) for i in range(2)]
                Kc = [sba(f"Kc{i}", [128, 2 * NT], BF16) for i in range(2)]
                Vc = [sba(f"Vc{i}", [128, 2 * NT], BF16) for i in range(2)]
                slots = Ring(pa, nc, "slot", 6, [128, NT], BF16)
                NVT = 69
                Vt = [sba(f"Vt{i}", [128, NVT + 3, 128], BF16) for i in range(1)]
                oacc = sba("oacc", [128, NT], F32)
                dacc = sba("dacc", [128, NT], F32)
                ering = Ring(pa, nc, "er", 3, [128, 512], BF16)
                ohout = Ring(pa, nc, "oho", 1, [128, NT], BF16)
                tiles = []
                for d_ in (1, 4, 16):
                    for r in range(d_):
                        for bb in range(-1, 16 // d_):
                            tiles.append((d_, r, bb))
                tidx = {t: i for i, t in enumerate(tiles)}
                assert len(tiles) == NVT

                def att_head(h):
                    hb = (h % 2) if (OPT & 2) else 0
                    q_, Kc_, Vc_, Vt_ = qa[hb], Kc[hb], Vc[hb], Vt[0]
                    K_ = lambda nm: ("att", nm, (0 if nm == "Vt" else hb))
                    S.dma("sp", lambda e: e.dma_start(out=q_[:], in_=aqs[:, h, :]), writes=[K_("q")])
                    S.dma("sp", lambda e: e.dma_start(out=Kc_[:, NT:2 * NT], in_=kv[:, h * NT:(h + 1) * NT]), writes=[K_("Kown")])
                    S.dma("sp", lambda e: e.dma_start(out=Vc_[:, NT:2 * NT], in_=kv[:, (8 + h) * NT:(9 + h) * NT]), writes=[K_("Vown")])
                    for which, dst, key in ((0, Kc_, K_("Kprev")), (1, Vc_, K_("Vprev"))):
                        banks = [ps3.next() for _ in range(4)]
                        for r in range(NCORE):
                            sl, slk = slots.next()
                            S.dma("sp", lambda e, sl=sl, r=r, which=which: e.dma_start(
                                out=sl[:], in_=kvg[r * 128:(r + 1) * 128, (which * 8 + h) * NT:(which * 8 + h + 1) * NT]), reads=["kvg"], writes=[slk])
                            for ct in range(4):
                                bt, bk = banks[ct]
                                S.op("pe", lambda e, bt=bt, sl=sl, r=r, ct=ct: e.matmul(bt[:], lhsT=ohm[:, r, :], rhs=sl[:, ct * 512:(ct + 1) * 512],
                                                                                     start=(r == 0), stop=(r == NCORE - 1)),
                                     reads=[slk, "ohm"], writes=[bk])
                        for ct in range(4):
                            bt, bk = banks[ct]
                            if ct % 2 == 0:
                                S.op("act", lambda e, bt=bt, ct=ct, dst=dst: e.activation(out=dst[:, ct * 512:(ct + 1) * 512], in_=bt[:], func=AF.Copy), reads=[bk], writes=[key])
                            else:
                                S.op("dve", lambda e, bt=bt, ct=ct, dst=dst: e.tensor_copy(out=dst[:, ct * 512:(ct + 1) * 512], in_=bt[:]), reads=[bk], writes=[key])
                    for t0 in range(0, NVT, 4):
                        ptb, ptbk = pb3.next()
                        grp = tiles[t0:t0 + 4]

                        def trv(e, ptb=ptb, grp=grp):
                            last = None
                            for j, (d_, r, bb) in enumerate(grp):
                                st_ = NT + bb * 128 * d_ + r
                                last = e.transpose(ptb[:, j * 128:(j + 1) * 128], Vc_[:, st_: st_ + 127 * d_ + 1: d_], ident_bf[:])
                            return last
                        S.op("pe", trv, reads=[K_("Vown"), K_("Vprev")], writes=[ptbk])
                        ng = len(grp)
                        if (t0 // 4) % 2 == 0:
                            S.op("act", lambda e, ptb=ptb, t0=t0, ng=ng: e.activation(out=Vt_[:, t0:t0 + ng, :], in_=ptb[:, 0:ng * 128].rearrange("p (a t) -> p a t", a=ng), func=AF.Copy),
                                 reads=[ptbk], writes=[K_("Vt")])
                        else:
                            S.op("dve", lambda e, ptb=ptb, t0=t0, ng=ng: e.tensor_copy(out=Vt_[:, t0:t0 + ng, :], in_=ptb[:, 0:ng * 128].rearrange("p (a t) -> p a t", a=ng)),
                                 reads=[ptbk], writes=[K_("Vt")])
                    S.op("pool", lambda e: e.memset(oacc[:], 0.0), writes=["oacc"])
                    S.op("pool", lambda e: e.memset(dacc[:], 0.0), writes=["dacc"])
                    pairs = []
                    for nb in range(0, 16, 2):
                        pairs.append((1, [(0, nb), (0, nb + 1)], "fn" if nb == 0 else "nn"))
                    for nb in range(4):
                        for r in range(0, 4, 2):
                            pairs.append((4, [(r, nb), (r + 1, nb)], "ff" if nb == 0 else "nn"))
                    for r in range(0, 16, 2):
                        pairs.append((16, [(r, 0), (r + 1, 0)], "ff"))
                    pvq = []

                    def do_pair_S(d_, units, mname):
                        pS, pSk = ps3.next()

                        def fs(e, pS=pS, d_=d_, units=units):
                            last = None
                            for u, (r, nb) in enumerate(units):
                                q0 = nb * 128 * d_ + r
                                qsl = q_[:, q0: q0 + 127 * d_ + 1: d_]
                                for half, bb in enumerate((nb - 1, nb)):
                                    k0 = NT + bb * 128 * d_ + r
                                    last = e.matmul(pS[:, half * 256 + u * 128: half * 256 + (u + 1) * 128], lhsT=Kc_[:, k0: k0 + 127 * d_ + 1: d_], rhs=qsl, start=True, stop=True)
                            return last
                        S.op("pe", fs, reads=[K_("q"), K_("Kown"), K_("Kprev")], writes=[pSk])
                        et, etk = ering.next()
                        S.op("act", lambda e, et=et, pS=pS: e.activation(out=et[:], in_=pS[:], func=AF.Exp, scale=128 ** -0.5), reads=[pSk], writes=[etk])
                        S.op("dve", lambda e, et=et, mname=mname: e.tensor_tensor(out=et[:], in0=et[:], in1=mk[mname][:], op=ALU.mult), reads=[etk, ("mk", mname)], writes=[etk])
                        pvq.append((d_, units, et, etk))

                    def do_pair_PV(d_, units, et, etk):
                        pO, pOk = ps3.next()

                        def fo2(e, pO=pO, et=et, d_=d_, units=units):
                            last = None
                            for u, (r, nb) in enumerate(units):
                                for half, bb in enumerate((nb - 1, nb)):
                                    last = e.matmul(pO[:, u * 128:(u + 1) * 128], lhsT=Vt_[:, tidx[(d_, r, bb)], :], rhs=et[:, half * 256 + u * 128: half * 256 + (u + 1) * 128],
                                                    start=(half == 0), stop=(half == 1))
                            for half in range(2):
                                last = e.matmul(pO[:, 256:512], lhsT=ones_bf[:], rhs=et[:, half * 256:(half + 1) * 256], start=(half == 0), stop=(half == 1))
                            return last
                        S.op("pe", fo2, reads=[etk, K_("Vt")], writes=[pOk])
                        (r0, nb0) = units[0]
                        if d_ == 1:
                            oview = lambda a, nb0=nb0: a[:, nb0 * 128: nb0 * 128 + 256].rearrange("p (u i) -> p u i", u=2)
                        else:
                            oview = lambda a, d_=d_, r0=r0, nb0=nb0: a[:, nb0 * 128 * d_:(nb0 + 1) * 128 * d_].rearrange("p (i r) -> p r i", r=d_)[:, r0:r0 + 2, :]
                        S.op("dve", lambda e, pO=pO, oview=oview: e.tensor_tensor(out=oview(oacc), in0=pO[:, 0:256].rearrange("p (u i) -> p u i", u=2), in1=oview(oacc), op=ALU.add),
                             reads=[pOk, "oacc"], writes=["oacc"])
                        S.op("dve", lambda e, pO=pO, oview=oview: e.tensor_tensor(out=oview(dacc), in0=pO[:, 256:512].rearrange("p (u i) -> p u i", u=2), in1=oview(dacc), op=ALU.add),
                             reads=[pOk, "dacc"], writes=["dacc"])

                    for pi, (d_, units, mname) in enumerate(pairs):
                        do_pair_S(d_, units, mname)
                        if len(pvq) > (1 if (OPT & 2) else 0):
                            do_pair_PV(*pvq.pop(0))
                    while pvq:
                        do_pair_PV(*pvq.pop(0))
                    if OPT & 4:
                        S.op("act", lambda e: e.activation(out=dacc[:], in_=dacc[:], func=AF.Ln), reads=["dacc"], writes=["dacc"])
                        S.op("act", lambda e: e.activation(out=dacc[:], in_=dacc[:], func=AF.Exp, scale=-1.0), reads=["dacc"], writes=["dacc"])
                    else:
                        S.op("dve", lambda e: e.reciprocal(out=dacc[:], in_=dacc[:]), reads=["dacc"], writes=["dacc"])
                    oo, ook = ohout.next()
                    S.op("dve", lambda e, oo=oo: e.tensor_tensor(out=oo[:], in0=oacc[:], in1=dacc[:], op=ALU.mult), reads=["oacc", "dacc"], writes=[ook])
                    S.dma("sp", lambda e, oo=oo: e.dma_start(out=oas[:, h, :], in_=oo[:]), reads=[ook], writes=[("oas", h)])

                if "noatt" not in dbg:
                    for h in range(8):
                        att_head(h)
                S.barrier_all()

            with ExitStack() as pc:
                def sbc(name, shape, dt):
                    return pc.enter_context(nc.sbuf_tensor("sc_" + name, list(shape), dt))
                C.sqring = Ring(pc, nc, "sq3", 3, [128, 2, 512], BF16)
                C.f32ring = Ring(pc, nc, "f32r3", 4, [128, 512], F32)
                gall = sbc("gall", [128, NCORE, GW], F32)
                S.dma("sp", lambda e: e.dma_start(out=gall[:], in_=gstg.rearrange("(r p) w -> p r w", p=128)), reads=["gstg"], writes=["gall"])
                S0 = sbc("S0", [128, 4, 256], F32)
                S0b = sbc("S0b", [128, 4, 256], BF16)
                tmpU = sbc("tmpU", [128, 4, 256], F32)
                aeff = sbc("aeff", [128, 4], F32)
                brt = Ring(pc, nc, "brt", 2, [128, 2, 512], BF16)
                obn = Ring(pc, nc, "obn", 2, [128, 2, 512], BF16)
                S.op("pool", lambda e: e.memset(S0[:], 0.0), writes=["S0"])
                gv = gall[:].rearrange("p r (h w) -> p r h w", h=4)
                for j in range(NCORE):
                    S.op("dve", lambda e, j=j: e.tensor_scalar(out=aeff[:], in0=gv[:, j, :, 256], scalar1=-1.0, scalar2=gmc[:, j:j + 1],
                                                               op0=ALU.add, op1=ALU.mult), reads=["gall", "gmc"], writes=["aeff"])
                    S.op("dve", lambda e: e.tensor_scalar(out=aeff[:], in0=aeff[:], scalar1=1.0, scalar2=None, op0=ALU.add), reads=["aeff"], writes=["aeff"])
                    S.op("pool", lambda e, j=j: e.tensor_scalar(out=tmpU[:], in0=gv[:, j, :, 0:256], scalar1=gmc[:, j:j + 1], scalar2=None, op0=ALU.mult),
                         reads=["gall", "gmc"], writes=["tmpU"])
                    S.op("dve", lambda e: e.tensor_tensor(out=S0[:], in0=S0[:], in1=aeff[:].unsqueeze(2).to_broadcast([128, 4, 256]), op=ALU.mult),
                         reads=["S0", "aeff"], writes=["S0"])
                    S.op("dve", lambda e: e.tensor_tensor(out=S0[:], in0=S0[:], in1=tmpU[:], op=ALU.add), reads=["S0", "tmpU"], writes=["S0"])
                S.op("dve", lambda e: e.tensor_copy(out=S0b[:], in_=S0[:]), reads=["S0"], writes=["S0b"])
                for h in range(4):
                    for tq in range(NT // 512):
                        cs4 = slice(tq * 512, (tq + 1) * 512)
                        for ec in range(2):
                            pcx, pck = ps3.next()
                            mm(S, pcx[:], [(S0b[:, h, ec * 128:(ec + 1) * 128], qE[:, h, cs4])], ["S0b", ("qE", h)], [pck])
                            S.op("dve", lambda e, pcx=pcx, h=h, ec=ec, cs4=cs4: e.tensor_tensor(out=oloc[:, 2 * h + ec, cs4], in0=pcx[:], in1=oloc[:, 2 * h + ec, cs4], op=ALU.add),
                                 reads=[pck, ("oloc", h)], writes=[("oloc", h)])
                        on, onk = obn.next()
                        bt_, btk = brt.next()
                        S.dma("sp", lambda e, bt_=bt_, h=h, cs4=cs4: e.dma_start(out=bt_[:], in_=brs[:, 2 * h:2 * h + 2, cs4]), writes=[btk])
                        rms_feature_major(S, C, smallc[:, 6:8], 2, oloc[:, 2 * h:2 * h + 2, :], lambda k, h=h: ("oloc", h), on, lambda k, onk=onk: onk, 512, tq * 512, dst_n0=0)
                        S.op("pool", lambda e, on=on, bt_=bt_: e.tensor_tensor(out=on[:], in0=on[:], in1=bt_[:], op=ALU.mult), reads=[onk, btk], writes=[onk])
                        S.dma("sp", lambda e, on=on, h=h, cs4=cs4: e.dma_start(out=obs[:, 2 * h:2 * h + 2, cs4], in_=on[:]), reads=[onk], writes=[("obs", h, tq)])
            S.barrier_all()

        if "p1only" not in dbg and "nop4" not in dbg:
          with ExitStack() as p4:
            def sb4(name, shape, dt):
                return p4.enter_context(nc.sbuf_tensor("s4_" + name, list(shape), dt))
            xT4 = sb4("xT", [128, KC, T], F32)
            xn4 = sb4("xn", [128, KC, T], BF16)
            C.ps = Ring(p4, nc, "ps4", 8, [128, 512], F32, psum=True)
            C.sqring = Ring(p4, nc, "sq4", 3, [128, 2, 512], BF16)
            C.f32ring = Ring(p4, nc, "f32r4", 4, [128, 512], F32)
            wring4 = Ring(p4, nc, "w4", 5, [128, 4096], BF16)
            hbuf4 = [sb4(f"h{i}", [128, 2, T], BF16) for i in range(2)]
            oab = sb4("oab", [128, 16, 512], BF16)
            gring = Ring(p4, nc, "gr4", 8, [128, 512], BF16)
            ostage = Ring(p4, nc, "ost4", 2, [128, D], F32)

            def wload4(view, a, b):
                wt, wk = wring4.next()
                dst = wt[:, 0:a * b].rearrange("p (a b) -> p a b", a=a)
                S.dma("pool", lambda e: e.dma_start(out=dst, in_=view), writes=[wk])
                return dst, wk
            xkey = lambda k: ("x", k)
            xnkey = lambda k: ("xn4", k)
            E4 = Ctx()
            E4.xT, E4.xn, E4.hbuf, E4.wload, E4.xkey, E4.xnkey = xT4, xn4, hbuf4, wload4, xkey, xnkey
            wav = w_aup.rearrange("(c p) d -> p c d", p=128)
            wbv = w_bup.rearrange("(c p) d -> p c d", p=128)
            wov = w_out.rearrange("(k p) d -> p k d", p=128)

            def tile4(tt):
                tok0 = tt * T
                for k in range(KC):
                    S.dma("sp", lambda e, k=k: e.dma_start(out=xT4[:, k, :], in_=x1s[:, k, tok0:tok0 + T]), writes=[xkey(k)])
                for n in range(NH):
                    cs = slice(n * 512, (n + 1) * 512)
                    S.dma("sp", lambda e, cs=cs: e.dma_start(out=oab[:, 0:8, :], in_=oas[:, :, tok0 + cs.start: tok0 + cs.stop]), writes=["oa"])
                    S.dma("sp", lambda e, cs=cs: e.dma_start(out=oab[:, 8:16, :], in_=obs[:, :, tok0 + cs.start: tok0 + cs.stop]), writes=["ob"])
                    for i2 in range(0, KC, 2):
                        wa, wak = wload4(wav[:, :, i2 * 128:(i2 + 2) * 128], 8, 256)
                        wb, wbk = wload4(wbv[:, :, i2 * 128:(i2 + 2) * 128], 8, 256)
                        for j in range(2):
                            i = i2 + j
                            pa_, pak = C.ps.next()
                            pb_, pbk = C.ps.next()
                            mm(S, pa_[:], [(wa[:, c, j * 128:(j + 1) * 128], oab[:, c, :]) for c in range(8)], [wak, "oa"], [pak])
                            mm(S, pb_[:], [(wb[:, c, j * 128:(j + 1) * 128], oab[:, 8 + c, :]) for c in range(8)], [wbk, "ob"], [pbk])
                            ga_, gak = gring.next()
                            gb_, gbk = gring.next()
                            S.dma("sp", lambda e, ga_=ga_, i=i, cs=cs: e.dma_start(out=ga_[:], in_=gas[:, i, tok0 + cs.start: tok0 + cs.stop]), writes=[gak])
                            S.dma("sp", lambda e, gb_=gb_, i=i, cs=cs: e.dma_start(out=gb_[:], in_=gbs[:, i, tok0 + cs.start: tok0 + cs.stop]), writes=[gbk])
                            t1, t1k = C.f32ring.next()
                            t2, t2k = C.f32ring.next()
                            S.op("dve", lambda e, t1=t1, pa_=pa_, ga_=ga_: e.tensor_tensor(out=t1[:], in0=pa_[:], in1=ga_[:], op=ALU.mult), reads=[pak, gak], writes=[t1k])
                            S.op("dve", lambda e, t2=t2, pb_=pb_, gb_=gb_: e.tensor_tensor(out=t2[:], in0=pb_[:], in1=gb_[:], op=ALU.mult), reads=[pbk, gbk], writes=[t2k])
                            S.op("dve", lambda e, t1=t1, t2=t2, i=i, cs=cs: e.tensor_tensor(out=xn4[:, i, cs], in0=t1[:], in1=t2[:], op=ALU.add), reads=[t1k, t2k], writes=[xnkey(i)])
                for i2 in range(0, KC, 2):
                    wo, wok = wload4(wov[:, :, i2 * 128:(i2 + 2) * 128], KC, 256)
                    for j in range(2):
                        i = i2 + j
                        for n in range(NH):
                            cs = slice(n * 512, (n + 1) * 512)
                            py, pyk = C.ps.next()
                            mm(S, py[:], [(wo[:, k, j * 128:(j + 1) * 128], xn4[:, k, cs]) for k in range(KC)], [wok] + [xnkey(k) for k in range(KC)], [pyk])
                            S.op("dve", lambda e, py=py, i=i, cs=cs: e.tensor_tensor(out=xT4[:, i, cs], in0=py[:], in1=xT4[:, i, cs], op=ALU.add), reads=[pyk, xkey(i)], writes=[xkey(i)])
                if debug:
                    for k in range(KC):
                        S.dma("sp", lambda e, k=k: e.dma_start(out=x2s[:, k, tok0:tok0 + T], in_=xT4[:, k, :]), reads=[xkey(k)], writes=[("x2s", k, tt)])
                for n in range(NH):
                    rms_feature_major(S, C, gcols[:, 2 * KC:3 * KC], KC, xT4, xkey, xn4, xnkey, 512, n * 512)
                ffn_block(S, C, E4, w_g2, w_u2, w_d2)
                for sub in range(T // 128):
                    og, ogk = ostage.next()
                    for k4 in range(0, KC, 4):
                        pt, pk = C.ps.next()

                        def tr(e, pt=pt, k4=k4, sub=sub):
                            last = None
                            for j in range(4):
                                last = e.transpose(pt[:, j * 128:(j + 1) * 128], xT4[:, k4 + j, sub * 128:(sub + 1) * 128], ident[:])
                            return last
                        S.op("pe", tr, reads=[xkey(k4 + j) for j in range(4)], writes=[pk])
                        if (k4 // 4) % 2 == 0:
                            S.op("act", lambda e, pt=pt, og=og, k4=k4: e.activation(out=og[:, k4 * 128:(k4 + 4) * 128], in_=pt[:], func=AF.Copy), reads=[pk], writes=[ogk])
                        else:
                            S.op("dve", lambda e, pt=pt, og=og, k4=k4: e.tensor_copy(out=og[:, k4 * 128:(k4 + 4) * 128], in_=pt[:]), reads=[pk], writes=[ogk])
                    S.dma("sp", lambda e, og=og, sub=sub: e.dma_start(out=out[tok0 + sub * 128: tok0 + (sub + 1) * 128, :], in_=og[:]), reads=[ogk], writes=[("out", tt, sub)])

            for tt in range(NTT):
                tile4(tt)
        S.emit(st)
    return nc


def _consts():
    ident = np.eye(128, dtype=np.float32)
    ropeP = np.zeros((128, 128), np.float32)
    for m in range(16):
        ropeP[m + 16, m] = 1.0
    for m in range(16, 32):
        ropeP[m - 16, m] = 1.0
    half = 16
    inv = np.power(np.float32(500000.0), -(np.arange(half, dtype=np.float32) * np.float32(2.0) / np.float32(32))).astype(np.float32)
    inv_col = np.zeros((128, 1), np.float32)
    inv_col[0:16, 0] = inv
    inv_col[16:32, 0] = inv
    sgn = np.zeros((128, 1), np.float32)
    sgn[0:16] = -1.0
    sgn[16:32] = 1.0
    kk = np.arange(128)[:, None]
    qq = np.arange(128)[None, :]
    mprev = (kk >= qq).astype(np.float32)
    mcur = (kk <= qq).astype(np.float32)
    return dict(ident=ident, ropeP=ropeP, inv_col=inv_col, sgn_col=sgn, mprev=mprev, mcur=mcur)


def make_in_maps(inputs):
    f = lambda a: np.ascontiguousarray(np.asarray(a))
    x = f(inputs["x"])[0]
    pos = f(inputs["positions"])
    col = lambda v: np.ascontiguousarray(f(v)[0].reshape(-1, 128).T)
    common = dict(
        ffn1_w_gate=f(inputs["ffn1_w_gate"])[0], ffn1_w_up=f(inputs["ffn1_w_up"])[0], ffn1_w_down=f(inputs["ffn1_w_down"])[0],
        ffn2_w_gate=f(inputs["ffn2_w_gate"])[0], ffn2_w_up=f(inputs["ffn2_w_up"])[0], ffn2_w_down=f(inputs["ffn2_w_down"])[0],
        w_in=f(inputs["w_in"])[0], w_a_up=f(inputs["w_a_up"])[0], w_b_up=f(inputs["w_b_up"])[0], w_out=f(inputs["w_out"])[0],
        g_ffn1=col(inputs["ffn1_norm"]), g_mix=col(inputs["mix_norm"]), g_ffn2=col(inputs["ffn2_norm"]),
        g_aq=col(inputs["a_q_norm"]), g_ak=col(inputs["a_k_norm"]),
        gate_w2=f(inputs["b_gate_w2"])[0], gate_b=col(inputs["b_gate_bias"]), g_bout=col(inputs["b_out_norm"]),
    )
    common.update(_consts())
    maps = []
    for c in range(NCORE):
        m = dict(common)
        m["x"] = np.ascontiguousarray(x[c * NT:(c + 1) * NT])
        m["pos"] = np.ascontiguousarray(pos[:, c * NT:(c + 1) * NT]).astype(np.int32)
        oh = np.zeros((128, NCORE), np.float32)
        if c > 0:
            oh[:, c - 1] = 1.0
        m["onehot"] = oh
        m["hasprev"] = np.full((128, 1), 1.0 if c > 0 else 0.0, np.float32)
        gm = np.zeros((128, NCORE), np.float32)
        gm[:, :c] = 1.0
        m["gmask"] = gm
        maps.append(m)
    return maps


def kernel(**inputs):
    nc = build_program(DEBUG)
    maps = make_in_maps(inputs)
    res = run_bass_kernel_spmd(nc, maps, core_ids=list(range(NCORE)))
    if DEBUG:
        return res
    outp = np.concatenate([res.results[c]["out"] for c in range(NCORE)], axis=0)
    return outp.reshape(1, SEQ, D).astype(np.float32)
```

```python
import os
from contextlib import ExitStack
import numpy as np
import concourse.bass as bass
import concourse.mybir as mybir
from concourse.bass_utils import run_bass_kernel_spmd

F32 = mybir.dt.float32
BF16 = mybir.dt.bfloat16
I32 = mybir.dt.int32
AF = mybir.ActivationFunctionType
ALU = mybir.AluOpType

NCORE = 8
SEQ = 16384
D = 2048
DFF = 5632
NT = SEQ // NCORE
KC = D // 128
T = 1024
NTT = NT // T
NH = T // 512
INW = 10256
EPS = 1e-6
TWO_PI = 6.283185307179586
CW1, CW2, CW3 = 6.28125, 0.0019350052, 3.019916e-07
PI = 3.141592653589793

DEBUG = os.environ.get("MK_DEBUG", "")
OPT = int(os.environ.get("MK_OPT", "3"))


class Sched:
    ENG = ("pe", "act", "dve", "pool", "sp")
    DMA_Q = ("sp", "act", "pool")
    NRING = 8

    def __init__(self, nc):
        self.nc = nc
        self.streams = {e: [] for e in self.ENG}
        self.nsem = 0
        self.eng_sem = {}
        self.eng_cnt = {}
        for e in ("pe", "act", "dve", "pool"):
            self.eng_sem[e] = self._new_sem()
            self.eng_cnt[e] = 0
        self.ring = {q: [self._new_sem() for _ in range(self.NRING)] for q in self.DMA_Q}
        self.ring_cnt = {q: [0] * self.NRING for q in self.DMA_Q}
        self.ring_pos = {q: 0 for q in self.DMA_Q}
        self.waited = {e: {} for e in self.ENG}
        self.last_w = {}
        self.readers = {}
        self.nops = 0
        self.cc_toks = []
        self.relay_fn = None

    def _new_sem(self):
        i = self.nsem
        self.nsem += 1
        return i

    def _need(self, eng, reads, writes, skip_sem=None):
        need = {}

        def add(tok):
            if tok is None:
                return
            s, v = tok
            if need.get(s, 0) < v:
                need[s] = v

        for r in reads:
            add(self.last_w.get(r))
        for w in writes:
            add(self.last_w.get(w))
            for t in self.readers.get(w, ()):
                add(t)
        out = []
        for s, v in need.items():
            if skip_sem is not None and s == skip_sem:
                continue
            if self.waited[eng].get(s, 0) >= v:
                continue
            self.waited[eng][s] = v
            out.append((s, v))
        return out

    def _commit(self, tok, reads, writes):
        for r in reads:
            lst = self.readers.setdefault(r, [])
            lst.append(tok)
            if len(lst) > 24:
                best = {}
                for s, v in lst:
                    if best.get(s, 0) < v:
                        best[s] = v
                self.readers[r] = list(best.items())
        for w in writes:
            self.last_w[w] = tok
            self.readers[w] = []

    def op(self, eng, fn, reads=(), writes=()):
        reads = list(reads)
        writes = list(writes)
        sem = self.eng_sem[eng]
        waits = self._need(eng, reads, writes, skip_sem=(sem if eng == "pe" else None))
        self.eng_cnt[eng] += 1
        tok = (sem, self.eng_cnt[eng])
        self.streams[eng].append(("op", waits, fn, sem, 1))
        self._commit(tok, reads, writes)
        self.nops += 1
        return tok

    def dma(self, q, fn, reads=(), writes=(), inc=16):
        reads = list(reads)
        writes = list(writes)
        pos = self.ring_pos[q]
        self.ring_pos[q] = (pos + 1) % self.NRING
        sem = self.ring[q][pos]
        waits = self._need(q, reads, writes)
        prev = self.ring_cnt[q][pos]
        if prev > 0 and self.waited[q].get(sem, 0) < prev:
            self.waited[q][sem] = prev
            waits.append((sem, prev))
        self.ring_cnt[q][pos] = prev + inc
        tok = (sem, prev + inc)
        self.streams[q].append(("op", waits, fn, sem, inc))
        self._commit(tok, reads, writes)
        self.nops += 1
        return tok

    def cc(self, fn, reads=(), writes=()):
        reads = list(reads)
        writes = list(writes)
        sem = self._new_sem()
        waits = self._need("pool", reads, writes)
        tok = (sem, 1)
        self.streams["pool"].append(("op", waits, fn, sem, 1))
        self._commit(tok, reads, writes)
        self.cc_toks.append(tok)
        return tok

    def wait_all_dma(self):
        for q in self.DMA_Q:
            waits = []
            for pos in range(self.NRING):
                v = self.ring_cnt[q][pos]
                sem = self.ring[q][pos]
                if v > 0 and self.waited[q].get(sem, 0) < v:
                    self.waited[q][sem] = v
                    waits.append((sem, v))
            if waits:
                self.streams[q].append(("wait", waits))

    def barrier_all(self):
        if self.cc_toks and self.relay_fn is not None:
            waits = []
            for s_, v_ in self.cc_toks:
                if self.waited["pool"].get(s_, 0) < v_:
                    self.waited["pool"][s_] = v_
                    waits.append((s_, v_))
            sem = self.eng_sem["pool"]
            self.eng_cnt["pool"] += 1
            self.streams["pool"].append(("op", waits, self.relay_fn, sem, 1))
            self.cc_toks = []
        toks = []
        for e in ("pe", "act", "dve", "pool"):
            if self.eng_cnt[e] > 0:
                toks.append((self.eng_sem[e], self.eng_cnt[e]))
        for q in self.DMA_Q:
            for pos in range(self.NRING):
                v = self.ring_cnt[q][pos]
                if v > 0:
                    toks.append((self.ring[q][pos], v))
        for e in self.ENG:
            waits = []
            for s, v in toks:
                if e in self.eng_sem and s == self.eng_sem[e]:
                    continue
                if self.waited[e].get(s, 0) < v:
                    self.waited[e][s] = v
                    waits.append((s, v))
            if waits:
                self.streams[e].append(("wait", waits))
        self.last_w = {}
        self.readers = {}

    def emit(self, stack):
        nc = self.nc
        self.wait_all_dma()
        sems = [stack.enter_context(nc.semaphore(f"s{i}")) for i in range(self.nsem)]
        streams = self.streams

        def run(engobj, items):
            for it in items:
                if it[0] == "wait":
                    for s, v in it[1]:
                        engobj.wait_ge(sems[s], v)
                else:
                    _, waits, fn, sem, inc = it
                    for s, v in waits:
                        engobj.wait_ge(sems[s], v)
                    ins = fn(engobj)
                    ins.then_inc(sems[sem], inc)

        with nc.Block() as block:
            @block.tensor
            def _(e):
                run(e, streams["pe"])

            @block.scalar
            def _(e):
                run(e, streams["act"])

            @block.vector
            def _(e):
                run(e, streams["dve"])

            @block.gpsimd
            def _(e):
                run(e, streams["pool"])

            @block.sync
            def _(e):
                run(e, streams["sp"])


class Ctx:
    pass


class SubRing:
    def __init__(self, ring, idxs):
        self.items = [(ring.tiles[i], (ring.name, i)) for i in idxs]
        self.i = 0

    def next(self):
        it = self.items[self.i]
        self.i = (self.i + 1) % len(self.items)
        return it


def mm(S, out_ap, pairs, reads, writes):
    pairs = list(pairs)

    def fn(e):
        last = None
        n = len(pairs)
        for i, (l, r) in enumerate(pairs):
            last = e.matmul(out_ap, lhsT=l, rhs=r, start=(i == 0), stop=(i == n - 1))
        return last
    return S.op("pe", fn, reads, writes)


class Ring:
    def __init__(self, st, nc, name, n, shape, dtype, psum=False):
        alloc = nc.psum_tensor if psum else nc.sbuf_tensor
        self.tiles = [st.enter_context(alloc(f"r_{name}{i}", shape, dtype)) for i in range(n)]
        self.name = name
        self.n = n
        self.i = 0

    def next(self):
        i = self.i
        self.i = (i + 1) % self.n
        return self.tiles[i], (self.name, i)


def rms_feature_major(S, C, gcol, nchunks, src, src_key, dst, dst_key, ncols, n0, dst_n0=None):
    cols = slice(n0, n0 + ncols)
    dcols = cols if dst_n0 is None else slice(dst_n0, dst_n0 + ncols)
    pss, pk = C.ps.next()
    for k2 in range(0, nchunks, 2):
        sq, sk = C.sqring.next()
        S.op("act", lambda e, sq=sq, k2=k2: e.activation(out=sq[:, :, 0:ncols], in_=src[:, k2:k2 + 2, cols], func=AF.Square),
             reads=[src_key(k2), src_key(k2 + 1)], writes=[sk])

        def fn(e, sq=sq, k2=k2):
            last = None
            for j in range(2):
                last = e.matmul(pss[:, 0:ncols], lhsT=C.ones_bf[:], rhs=sq[:, j, 0:ncols],
                                start=(k2 == 0 and j == 0), stop=(k2 + j == nchunks - 1))
            return last
        S.op("pe", fn, reads=[sk], writes=[pk])
    rt, rk = C.f32ring.next()
    S.op("act", lambda e: e.activation(out=rt[:, 0:ncols], in_=pss[:, 0:ncols], func=AF.Ln,
                                       scale=1.0 / (128 * nchunks), bias=C.eps_col[:, 0:1]),
         reads=[pk], writes=[rk])
    S.op("act", lambda e: e.activation(out=rt[:, 0:ncols], in_=rt[:, 0:ncols], func=AF.Exp, scale=-0.5), reads=[rk], writes=[rk])
    for k in range(nchunks):
        S.op("dve", lambda e, k=k: e.scalar_tensor_tensor(out=dst[:, k, dcols], in0=src[:, k, cols], scalar=gcol[:, k:k + 1],
                                                        in1=rt[:, 0:ncols], op0=ALU.mult, op1=ALU.mult),
             reads=[src_key(k), rk], writes=[dst_key(k)])


def ffn_block(S, C, E, wg_d, wu_d, wd_d):
    xT, xn, hbuf, wload, xkey, xnkey = E.xT, E.xn, E.hbuf, E.wload, E.xkey, E.xnkey
    wgv = wg_d.rearrange("(k p) f -> p k f", p=128)
    wuv = wu_d.rearrange("(k p) f -> p k f", p=128)
    wdv = wd_d.rearrange("(j p) d -> p j d", p=128)
    NG = DFF // 256

    def gu(g):
        wg, wgk = wload(wgv[:, :, g * 256:(g + 1) * 256], KC, 256)
        wu, wuk = wload(wuv[:, :, g * 256:(g + 1) * 256], KC, 256)
        hb = hbuf[g % 2]
        for j in range(2):
            for n in range(NH):
                cs = slice(n * 512, (n + 1) * 512)
                pg, pgk = C.ps.next()
                pu, puk = C.ps.next()
                mm(S, pg[:], [(wg[:, k, j * 128:(j + 1) * 128], xn[:, k, cs]) for k in range(KC)],
                   [wgk] + [xnkey(k) for k in range(KC)], [pgk])
                mm(S, pu[:], [(wu[:, k, j * 128:(j + 1) * 128], xn[:, k, cs]) for k in range(KC)],
                   [wuk] + [xnkey(k) for k in range(KC)], [puk])
                sg, sgk = C.f32ring.next()
                S.op("act", lambda e, sg=sg, pg=pg: e.activation(out=sg[:], in_=pg[:], func=AF.Silu), reads=[pgk], writes=[sgk])
                S.op("dve", lambda e, sg=sg, pu=pu, hb=hb, j=j, cs=cs: e.tensor_tensor(out=hb[:, j, cs], in0=pu[:], in1=sg[:], op=ALU.mult),
                     reads=[puk, sgk], writes=[("h", g % 2, j, n)])

    def down(g):
        wd, wdk = wload(wdv[:, 2 * g:2 * g + 2, :], 2, D)
        hb = hbuf[g % 2]
        for i in range(KC):
            for n in range(NH):
                cs = slice(n * 512, (n + 1) * 512)
                py, pyk = C.ps.next()
                mm(S, py[:], [(wd[:, j, i * 128:(i + 1) * 128], hb[:, j, cs]) for j in range(2)],
                   [wdk] + [("h", g % 2, j, n) for j in range(2)], [pyk])
                S.op("dve", lambda e, py=py, i=i, cs=cs: e.scalar_tensor_tensor(out=xT[:, i, cs], in0=py[:], scalar=0.5, in1=xT[:, i, cs],
                                                                          op0=ALU.mult, op1=ALU.add),
                     reads=[pyk, xkey(i)], writes=[xkey(i)])

    gu(0)
    for g in range(NG):
        if g + 1 < NG:
            gu(g + 1)
        down(g)


def build_program(debug=""):
    nc = bass.Bass("TRN2", target_bir_lowering=False)
    dbg = set(debug.split(",")) if debug else set()
    dbg_set = dbg
    C = Ctx()
    S = Sched(nc)

    def din(name, shape, dt=F32):
        return nc.dram_tensor(name, list(shape), dt, kind="ExternalInput").ap()

    def dscratch(name, shape=None, dt=None, dbg=False, p1out=False, ccout=False):
        kind = "ExternalOutput" if (dbg and debug) else "Internal"
        if p1out and "skipp1" in dbg_set:
            kind = "ExternalInput"
        if ccout and "nocc" in dbg_set:
            kind = "ExternalInput"
        return nc.dram_tensor(name, list(shape), dt, kind=kind).ap()

    x = din("x", [NT, D])
    pos = din("pos", [1, NT], I32)
    w_g1 = din("ffn1_w_gate", [D, DFF]); w_u1 = din("ffn1_w_up", [D, DFF]); w_d1 = din("ffn1_w_down", [DFF, D])
    w_g2 = din("ffn2_w_gate", [D, DFF]); w_u2 = din("ffn2_w_up", [D, DFF]); w_d2 = din("ffn2_w_down", [DFF, D])
    w_in = din("w_in", [D, INW])
    w_aup = din("w_a_up", [1024, D]); w_bup = din("w_b_up", [1024, D]); w_out = din("w_out", [D, D])
    g_ffn1 = din("g_ffn1", [128, KC]); g_mix = din("g_mix", [128, KC]); g_ffn2 = din("g_ffn2", [128, KC])
    g_aq = din("g_aq", [128, 1]); g_ak = din("g_ak", [128, 1])
    gate_w2 = din("gate_w2", [16, 512]); gate_b = din("gate_b", [128, 4]); g_bout = din("g_bout", [128, 2])
    ident_in = din("ident", [128, 128]); ropeP_in = din("ropeP", [128, 128])
    inv_in = din("inv_col", [128, 1]); sgn_in = din("sgn_col", [128, 1])
    mprev_in = din("mprev", [128, 128]); mcur_in = din("mcur", [128, 128])
    onehot_in = din("onehot", [128, NCORE]); hasprev_in = din("hasprev", [128, 1]); gmask_in = din("gmask", [128, NCORE])

    out = nc.dram_tensor("out", [NT, D], F32, kind="ExternalOutput").ap()

    x1s = dscratch("x1s", shape=[128, KC, NT], dt=F32, dbg=True, p1out=True)
    aqs = dscratch("aqs", shape=[128, 8, NT], dt=BF16, dbg=True, p1out=True)
    kv = dscratch("kv", shape=[128, 2 * 8 * NT], dt=BF16, dbg=True, p1out=True)
    bqs = dscratch("bqs", shape=[128, 4, NT], dt=BF16, dbg=True, p1out=True)
    bks = dscratch("bks", shape=[128, 4, NT], dt=BF16, dbg=True, p1out=True)
    bvs = dscratch("bvs", shape=[NT // 128, 128, 1024], dt=BF16, dbg=True, p1out=True)
    brs = dscratch("brs", shape=[128, 8, NT], dt=BF16, dbg=True, p1out=True)
    bzs = dscratch("bzs", shape=[16, NT], dt=F32, dbg=True, p1out=True)
    gas = dscratch("gas", shape=[128, 16, NT], dt=BF16, dbg=True, p1out=True)
    gbs = dscratch("gbs", shape=[128, 16, NT], dt=BF16, dbg=True, p1out=True)

    kvg = dscratch("kvg", shape=[NCORE * 128, 2 * 8 * NT], dt=BF16, ccout=True)
    oas = dscratch("oas", shape=[128, 8, NT], dt=BF16, dbg=True)
    obs = dscratch("obs", shape=[128, 8, NT], dt=BF16, dbg=True)
    GW = 4 * 264
    gst = dscratch("gst", shape=[128, GW], dt=F32, dbg=True)
    gstg = dscratch("gstg", shape=[NCORE * 128, GW], dt=F32, ccout=True)
    x2s = dscratch("x2s", shape=[128, KC, NT], dt=F32, dbg=True)

    with ExitStack() as st:
        def sb(name, shape, dt):
            return st.enter_context(nc.sbuf_tensor("s_" + name, list(shape), dt))

        ident = sb("ident", [128, 128], F32)
        ident_bf = sb("ident_bf", [128, 128], BF16)
        ones_bf = sb("ones_bf", [128, 128], BF16)
        ropeP = sb("ropeP", [128, 128], F32)
        gcols = sb("gcols", [128, 3 * KC], F32)
        smallc = sb("smallc", [128, 16], F32)
        C.ident = ident; C.ident_bf = ident_bf; C.ones_bf = ones_bf
        S.dma("sp", lambda e: e.dma_start(out=ident[:], in_=ident_in[:, :]), writes=["ident"])
        S.dma("sp", lambda e: e.dma_start(out=ropeP[:], in_=ropeP_in[:, :]), writes=["ropeP"])
        for i, g in enumerate((g_ffn1, g_mix, g_ffn2)):
            S.dma("sp", lambda e, i=i, g=g: e.dma_start(out=gcols[:, i * KC:(i + 1) * KC], in_=g[:, :]), writes=["gcols"])
        for i, a in enumerate((g_aq, g_ak, inv_in, sgn_in)):
            S.dma("sp", lambda e, i=i, a=a: e.dma_start(out=smallc[:, i:i + 1], in_=a[:, :]), writes=["smallc"])
        S.dma("sp", lambda e: e.dma_start(out=smallc[:, 5:6], in_=hasprev_in[:, :]), writes=["smallc"])
        S.dma("sp", lambda e: e.dma_start(out=smallc[:, 6:8], in_=g_bout[:, :]), writes=["smallc"])
        S.dma("sp", lambda e: e.dma_start(out=smallc[:, 8:12], in_=gate_b[:, :]), writes=["smallc"])
        S.op("dve", lambda e: e.memset(smallc[:, 4:5], EPS), writes=["smallc"], reads=["smallc"])
        S.op("dve", lambda e: e.memset(smallc[:, 12:13], 1.0), writes=["smallc"], reads=["smallc"])
        S.op("dve", lambda e: e.tensor_copy(out=ident_bf[:], in_=ident[:]), reads=["ident"], writes=["ident_bf"])
        S.op("dve", lambda e: e.memset(ones_bf[:], 1.0), writes=["ones_bf"])
        C.eps_col = smallc[:, 4:5]
        relay_t = sb("relay", [128, 8], F32)
        S.relay_fn = lambda e: e.memset(relay_t[:, 0:1], 0.0)
        S.barrier_all()

        with ExitStack() as p1:
          if "skipp1" not in dbg:
            def sb1(name, shape, dt):
                return p1.enter_context(nc.sbuf_tensor("s_" + name, list(shape), dt))

            xT = sb1("xT", [128, KC, T], F32)
            xn = sb1("xn", [128, KC, T], BF16)
            C.ps = Ring(p1, nc, "ps", 8, [128, 512], F32, psum=True)
            C.sqring = Ring(p1, nc, "sq", 3, [128, 2, 512], BF16)
            C.f32ring = Ring(p1, nc, "f32r", 4, [128, 512], F32)
            wring = Ring(p1, nc, "w", 6, [128, 4096], BF16)
            hbuf = [sb1(f"h{i}", [128, 2, T], BF16) for i in range(2)]
            stage = Ring(p1, nc, "stage", 2, [128, D], F32)
            obring = Ring(p1, nc, "ob", 4, [128, T], BF16)
            cosT = sb1("cosT", [128, T], F32)
            sinT = sb1("sinT", [128, T], F32)
            posi = sb1("posi", [128, T], I32)

            def wload(view, a, b):
                wt, wk = wring.next()
                dst = wt[:, 0:a * b].rearrange("p (a b) -> p a b", a=a)
                S.dma("pool", lambda e: e.dma_start(out=dst, in_=view), writes=[wk])
                return dst, wk

            xkey = lambda k: ("x", k)
            xnkey = lambda k: ("xn", k)
            E1 = Ctx()
            E1.xT, E1.xn, E1.hbuf, E1.wload, E1.xkey, E1.xnkey = xT, xn, hbuf, wload, xkey, xnkey

            def tile_body(tt):
                tok0 = tt * T
                for sub in range(T // 128):
                    stg, sk = stage.next()
                    S.dma("sp", lambda e, stg=stg, sub=sub: e.dma_start(out=stg[:], in_=x[tok0 + sub * 128: tok0 + (sub + 1) * 128, :]),
                          writes=[sk])
                    for k4 in range(0, KC, 4):
                        pt, pk = C.ps.next()

                        def tr(e, stg=stg, k4=k4, pt=pt):
                            last = None
                            for j in range(4):
                                last = e.transpose(pt[:, j * 128:(j + 1) * 128], stg[:, (k4 + j) * 128:(k4 + j + 1) * 128], ident[:])
                            return last
                        S.op("pe", tr, reads=[sk], writes=[pk])
                        eng = "act" if (k4 // 4) % 2 == 0 else "dve"
                        if eng == "act":
                            S.op("act", lambda e, pt=pt, k4=k4, sub=sub: e.activation(
                                out=xT[:, k4:k4 + 4, sub * 128:(sub + 1) * 128], in_=pt[:].rearrange("p (j t) -> p j t", j=4), func=AF.Copy),
                                reads=[pk], writes=[xkey(k4 + j) for j in range(4)])
                        else:
                            S.op("dve", lambda e, pt=pt, k4=k4, sub=sub: e.tensor_copy(
                                out=xT[:, k4:k4 + 4, sub * 128:(sub + 1) * 128], in_=pt[:].rearrange("p (j t) -> p j t", j=4)),
                                reads=[pk], writes=[xkey(k4 + j) for j in range(4)])
                if "norope" not in dbg:
                    rope_tables(tok0)

                for n in range(NH):
                    rms_feature_major(S, C, gcols[:, 0:KC], KC, xT, xkey, xn, xnkey, 512, n * 512)
                rest_of_tile(tt, tok0)

            def rope_tables(tok0):
                S.dma("sp", lambda e: e.dma_start(out=posi[:], in_=pos[0:1, tok0:tok0 + T].partition_broadcast(128)), writes=["posi"])
                ang = cosT
                S.op("dve", lambda e: e.tensor_copy(out=sinT[:], in_=posi[:]), reads=["posi"], writes=["sinT"])
                S.op("dve", lambda e: e.tensor_scalar(out=ang[:], in0=sinT[:], scalar1=smallc[:, 2:3], scalar2=None, op0=ALU.mult),
                     reads=["sinT"], writes=["cosT"])
                S.op("dve", lambda e: e.tensor_scalar(out=sinT[:], in0=ang[:], scalar1=1.0 / TWO_PI, scalar2=None, op0=ALU.mult),
                     reads=["cosT"], writes=["sinT"])
                S.op("dve", lambda e: e.tensor_copy(out=posi[:], in_=sinT[:]), reads=["sinT"], writes=["posi"])
                S.op("dve", lambda e: e.tensor_copy(out=sinT[:], in_=posi[:]), reads=["posi"], writes=["sinT"])
                for cw in (CW1, CW2, CW3):
                    S.op("dve", lambda e, cw=cw: e.scalar_tensor_tensor(out=ang[:], in0=sinT[:], scalar=-cw, in1=ang[:], op0=ALU.mult, op1=ALU.add),
                         reads=["sinT", "cosT"], writes=["cosT"])
                S.op("dve", lambda e: e.tensor_scalar(out=sinT[:], in0=ang[:], scalar1=PI, scalar2=-TWO_PI, op0=ALU.is_gt, op1=ALU.mult),
                     reads=["cosT"], writes=["sinT"])
                S.op("dve", lambda e: e.tensor_tensor(out=ang[:], in0=ang[:], in1=sinT[:], op=ALU.add), reads=["cosT", "sinT"], writes=["cosT"])
                S.op("dve", lambda e: e.tensor_scalar(out=sinT[:], in0=ang[:], scalar1=-PI, scalar2=TWO_PI, op0=ALU.is_lt, op1=ALU.mult),
                     reads=["cosT"], writes=["sinT"])
                S.op("dve", lambda e: e.tensor_tensor(out=ang[:], in0=ang[:], in1=sinT[:], op=ALU.add), reads=["cosT", "sinT"], writes=["cosT"])
                S.op("dve", lambda e: e.tensor_scalar(out=ang[:], in0=ang[:], scalar1=PI, scalar2=-PI, op0=ALU.min, op1=ALU.max),
                     reads=["cosT"], writes=["cosT"])
                S.op("act", lambda e: e.activation(out=sinT[:], in_=ang[:], func=AF.Sin, scale=smallc[:, 3:4]), reads=["cosT"], writes=["sinT"])
                S.op("dve", lambda e: e.tensor_scalar(out=ang[:], in0=ang[:], scalar1=PI / 2, scalar2=None, op0=ALU.add),
                     reads=["cosT", "sinT"], writes=["cosT"])
                tmpc, tck = C.f32ring.next()
                for hh in range(NH):
                    cs = slice(hh * 512, (hh + 1) * 512)
                    S.op("dve", lambda e, cs=cs: e.tensor_scalar(out=tmpc[:], in0=ang[:, cs], scalar1=PI, scalar2=-TWO_PI, op0=ALU.is_gt, op1=ALU.mult),
                         reads=["cosT"], writes=[tck])
                    S.op("dve", lambda e, cs=cs: e.tensor_tensor(out=ang[:, cs], in0=ang[:, cs], in1=tmpc[:], op=ALU.add), reads=["cosT", tck], writes=["cosT"])
                S.op("dve", lambda e: e.tensor_scalar(out=ang[:], in0=ang[:], scalar1=PI, scalar2=-PI, op0=ALU.min, op1=ALU.max),
                     reads=["cosT"], writes=["cosT"])
                S.op("act", lambda e: e.activation(out=cosT[:], in_=ang[:], func=AF.Sin), reads=["cosT"], writes=["cosT"])

            def rest_of_tile(tt, tok0):
                ffn_block(S, C, E1, w_g1, w_u1, w_d1)

                for k in range(KC):
                    S.dma("sp", lambda e, k=k: e.dma_start(out=x1s[:, k, tok0:tok0 + T], in_=xT[:, k, :]), reads=[xkey(k)], writes=[("x1s", k, tt)])

                if "ffn1" in dbg or "ffn1s" in dbg:
                    return
                for n in range(NH):
                    rms_feature_major(S, C, gcols[:, KC:2 * KC], KC, xT, xkey, xn, xnkey, 512, n * 512)

                w_inv = w_in.rearrange("(k p) f -> p k f", p=128)

                pend = []

                def advance(flush=False):
                    while True:
                        for ent in list(pend):
                            ent.pop(0)()
                            if not ent:
                                pend.remove(ent)
                        if not flush or not pend:
                            break

                def proj_chunk(col0, ncol, epilogue):
                    w, wk = wload(w_inv[:, :, col0:col0 + ncol], KC, ncol)
                    for j in range((ncol + 127) // 128):
                        m = min(128, ncol - j * 128)
                        for n in range(NH):
                            cs = slice(n * 512, (n + 1) * 512)
                            pt, pk = C.ps.next()
                            mm(S, pt[0:m, :], [(w[:, k, j * 128:j * 128 + m], xn[:, k, cs]) for k in range(KC)],
                               [wk] + [xnkey(k) for k in range(KC)], [pk])
                            advance()
                            st_ = epilogue(pt, pk, j, n)
                            pend.append(list(st_))

                def store_bf(dst_fn):
                    state = {}

                    def ep(pt, pk, j, n, func=None):
                        raise NotImplementedError
                    return ep

                def qk_epilogue(which, head):
                    gc = smallc[:, 0:1] if which == "q" else smallc[:, 1:2]
                    ot_box = {}

                    def ep(pt, pk, j, n):
                        cs = slice(n * 512, (n + 1) * 512)
                        box = {}

                        def stage1():
                            if n == 0:
                                ot_box["t"] = obring.next()
                            box["ot"] = ot_box["t"]
                            sq, sk = C.sqring.next()
                            S.op("act", lambda e: e.activation(out=sq[:, 0, :], in_=pt[:], func=AF.Square), reads=[pk], writes=[sk])
                            pss, psk = C.ps.next()
                            mm(S, pss[:], [(ones_bf[:], sq[:, 0, :])], [sk], [psk])
                            box["pss"] = (pss, psk)

                        def stage2():
                            pss, psk = box["pss"]
                            rt, rk = C.f32ring.next()
                            S.op("act", lambda e: e.activation(out=rt[:], in_=pss[:], func=AF.Ln, scale=1.0 / 128, bias=C.eps_col), reads=[psk], writes=[rk])
                            S.op("act", lambda e: e.activation(out=rt[:], in_=rt[:], func=AF.Exp, scale=-0.5), reads=[rk], writes=[rk])
                            qn, qnk = C.f32ring.next()
                            S.op("dve", lambda e: e.scalar_tensor_tensor(out=qn[:], in0=pt[:], scalar=gc, in1=rt[:], op0=ALU.mult, op1=ALU.mult),
                                 reads=[pk, rk], writes=[qnk])
                            pr, prk = C.ps.next()
                            mm(S, pr[:], [(ropeP[:], qn[:])], [qnk], [prk])
                            box["r"] = (rt, rk, qn, qnk, pr, prk)

                        def stage3():
                            rt, rk, qn, qnk, pr, prk = box["r"]
                            ot, ok = box["ot"]
                            S.op("dve", lambda e: e.tensor_tensor(out=rt[:], in0=pr[:], in1=sinT[:, cs], op=ALU.mult), reads=[prk, "sinT", rk], writes=[rk])
                            S.op("pool", lambda e: e.tensor_tensor(out=qn[:], in0=qn[:], in1=cosT[:, cs], op=ALU.mult), reads=[qnk, "cosT"], writes=[qnk])
                            S.op("dve", lambda e: e.tensor_tensor(out=ot[:, cs], in0=qn[:], in1=rt[:], op=ALU.add), reads=[qnk, rk], writes=[ok])
                            if n == NH - 1:
                                if which == "q":
                                    dst = aqs[:, head, tok0:tok0 + T]
                                else:
                                    dst = kv[:, head * NT + tok0: head * NT + tok0 + T]
                                S.dma("sp", lambda e: e.dma_start(out=dst, in_=ot[:]), reads=[ok], writes=[("dr", which, head, tt)])
                        return [stage1, stage2, stage3]
                    return ep

                def plain_epilogue(dst_fn, func=None, scale=1.0):
                    ot_box = {}

                    def ep(pt, pk, j, n):
                      def stage():
                        cs = slice(n * 512, (n + 1) * 512)
                        if n == 0:
                            ot_box["t"] = obring.next()
                        ot, ok = ot_box["t"]
                        if func is None:
                            if scale == 1.0 and (j + n) % 2 == 0:
                                S.op("dve", lambda e: e.tensor_copy(out=ot[:, cs], in_=pt[:]), reads=[pk], writes=[ok])
                            else:
                                S.op("act", lambda e: e.activation(out=ot[:, cs], in_=pt[:], func=AF.Copy, scale=scale), reads=[pk], writes=[ok])
                        else:
                            S.op("act", lambda e: e.activation(out=ot[:, cs], in_=pt[:], func=func), reads=[pk], writes=[ok])
                        if n == NH - 1:
                            dst = dst_fn(j)
                            S.dma("sp", lambda e: e.dma_start(out=dst, in_=ot[:]), reads=[ok], writes=[("dr2", id(dst_fn), j, tt)])
                      return [stage]
                    return ep

                for hp in range(0, 8, 2):
                    eps_ = [qk_epilogue("q", hp), qk_epilogue("q", hp + 1)]
                    proj_chunk(hp * 128, 256, lambda pt, pk, j, n: eps_[j](pt, pk, j, n))
                for hp in range(0, 8, 2):
                    eps_ = [qk_epilogue("k", hp), qk_epilogue("k", hp + 1)]
                    proj_chunk(1024 + hp * 128, 256, lambda pt, pk, j, n: eps_[j](pt, pk, j, n))
                for hp in range(0, 8, 2):
                    proj_chunk(2048 + hp * 128, 256, plain_epilogue(
                        lambda j, hp=hp: kv[:, 8 * NT + (hp + j) * NT + tok0: 8 * NT + (hp + j) * NT + tok0 + T]))
                for c2 in range(0, 4, 2):
                    proj_chunk(3072 + c2 * 128, 256, plain_epilogue(lambda j, c2=c2: bqs[:, c2 + j, tok0:tok0 + T], scale=128 ** -0.5))
                for c2 in range(0, 4, 2):
                    proj_chunk(3584 + c2 * 128, 256, plain_epilogue(lambda j, c2=c2: bks[:, c2 + j, tok0:tok0 + T]))
                advance(flush=True)
                for c2 in range(0, 8, 2):
                    w, wk = wload(w_inv[:, :, 4096 + c2 * 128: 4096 + c2 * 128 + 256], KC, 256)
                    for sub in range(T // 128):
                        pt, pk = C.ps.next()
                        mm(S, pt[:, 0:256], [(xn[:, k, sub * 128:(sub + 1) * 128], w[:, k, :]) for k in range(KC)],
                           [wk] + [xnkey(k) for k in range(KC)], [pk])
                        ot, ok = obring.next()
                        if sub % 2 == 0:
                            S.op("dve", lambda e, ot=ot, pt=pt: e.tensor_copy(out=ot[:, 0:256], in_=pt[:, 0:256]), reads=[pk], writes=[ok])
                        else:
                            S.op("act", lambda e, ot=ot, pt=pt: e.activation(out=ot[:, 0:256], in_=pt[:, 0:256], func=AF.Copy), reads=[pk], writes=[ok])
                        S.dma("sp", lambda e, ot=ot, sub=sub, c2=c2: e.dma_start(out=bvs[tt * (T // 128) + sub, :, c2 * 128:c2 * 128 + 256], in_=ot[:, 0:256]),
                              reads=[ok], writes=[("bvs", tt, sub, c2)])
                for c2 in range(0, 8, 2):
                    proj_chunk(5120 + c2 * 128, 256, plain_epilogue(lambda j, c2=c2: brs[:, c2 + j, tok0:tok0 + T], func=AF.Silu))
                def bz_ep(pt, pk, j, n):
                    def stage():
                        zt, zk = C.f32ring.next()
                        S.op("dve", lambda e: e.tensor_copy(out=zt[0:16, :], in_=pt[0:16, :]), reads=[pk], writes=[zk])
                        S.dma("sp", lambda e: e.dma_start(out=bzs[:, tok0 + n * 512: tok0 + (n + 1) * 512], in_=zt[0:16, :]), reads=[zk], writes=[("bzs", tt, n)])
                    return [stage]
                proj_chunk(6144, 16, bz_ep)
                for c2 in range(0, 16, 2):
                    proj_chunk(6160 + c2 * 128, 256, plain_epilogue(lambda j, c2=c2: gas[:, c2 + j, tok0:tok0 + T], func=AF.Sigmoid))
                for c2 in range(0, 16, 2):
                    proj_chunk(8208 + c2 * 128, 256, plain_epilogue(lambda j, c2=c2: gbs[:, c2 + j, tok0:tok0 + T], func=AF.Sigmoid))
                advance(flush=True)

            for tt in range(NTT if "ffn1s" not in dbg else 1):
                tile_body(tt)
            S.barrier_all()

        rg = [list(range(NCORE))]
        if "nocc" not in dbg and "p1only" not in dbg:
            S.cc(lambda e: e.collective_compute("AllGather", ALU.bypass, replica_groups=rg,
                                                ins=[kv.opt()], outs=[kvg.opt()], dma_qos="P2"),
                 reads=[], writes=["kvg"])
        if "p1only" not in dbg:
          with ExitStack() as p3:
            def sb3(name, shape, dt):
                return p3.enter_context(nc.sbuf_tensor("s3_" + name, list(shape), dt))
            ps3 = Ring(p3, nc, "ps3", 6, [128, 512], F32, psum=True)
            pb3 = Ring(p3, nc, "pb3", 2, [128, 1024], BF16, psum=True)
            C.ps = ps3
            oloc = sb3("oloc", [128, 8, NT], F32)
            qE = sb3("qE", [128, 4, NT], BF16)
            gstt = sb3("gstt", [128, GW], F32)
            maskU = sb3("maskU", [128, 128], F32)
            mprev = sb3("mprevf", [128, 128], F32)
            onesf = sb3("onesf", [128, 128], F32)
            ohc = sb3("ohc", [128, NCORE], F32)
            gmc = sb3("gmc", [128, NCORE], F32)
            S.dma("sp", lambda e: e.dma_start(out=maskU[:], in_=mcur_in[:, :]), writes=["maskU"])
            S.dma("sp", lambda e: e.dma_start(out=mprev[:], in_=mprev_in[:, :]), writes=["mprev"])
            S.dma("sp", lambda e: e.dma_start(out=ohc[:], in_=onehot_in[:, :]), writes=["ohc"])
            S.dma("sp", lambda e: e.dma_start(out=gmc[:], in_=gmask_in[:, :]), writes=["gmc"])
            S.op("pool", lambda e: e.memset(onesf[:], 1.0), writes=["onesf"])

            with ExitStack() as pg:
                def sbg(name, shape, dt):
                    return pg.enter_context(nc.sbuf_tensor("sg_" + name, list(shape), dt))
                w2sb = sbg("w2sb", [16, 512], F32)
                bzT = sbg("bzT", [16, NT], F32)
                negb = sbg("negb", [128, 4], F32)
                S.dma("sp", lambda e: e.dma_start(out=w2sb[:], in_=gate_w2[:, :]), writes=["w2sb"])
                S.dma("sp", lambda e: e.dma_start(out=bzT[:], in_=bzs[:, :]), writes=["bzT"])
                S.op("dve", lambda e: e.tensor_scalar(out=negb[:], in0=smallc[:, 8:12], scalar1=-1.0, scalar2=None, op0=ALU.mult), writes=["negb"])
                vtok = sbg("vtok", [128, NT // 128, 1024], BF16)
                for n_ in range(NT // 128):
                    S.dma("sp", lambda e, n_=n_: e.dma_start(out=vtok[:, n_, :], in_=bvs[n_, :, :]), writes=[("vtok", n_)])
                qh = [sbg(f"qh{i}", [128, NT], BF16) for i in range(2)]
                kh = [sbg(f"kh{i}", [128, NT], BF16) for i in range(2)]
                lsp = [sbg(f"lsp{i}", [128, NT], F32) for i in range(2)]
                csb = [sbg(f"cs{i}", [128, NT], F32) for i in range(2)]
                Sst = [sbg(f"S{i}", [128, 256], F32) for i in range(2)]
                Sbf = [sbg(f"Sbf{i}", [128, 256], BF16) for i in range(2)]
                Bp = [sbg(f"Bp{i}", [128, NT // 128 + 1], F32) for i in range(2)]
                e32 = Ring(pg, nc, "ge32", 14, [128, 128], F32)
                b16 = Ring(pg, nc, "gb16", 22, [128, 128], BF16)

                def gla_bufs(h):
                    hb = h % 2
                    return qh[hb], kh[hb], lsp[hb], csb[hb], Sst[hb], Sbf[hb], Bp[hb], (lambda nm: ("gla", nm, hb))

                def gla_head(h):
                    q_, k_, l_, c_, S_, Sb_, B_, K_ = gla_bufs(h)
                    S.dma("sp", lambda e: e.dma_start(out=q_[:], in_=bqs[:, h, :]), writes=[K_("q")])
                    S.dma("sp", lambda e: e.dma_start(out=k_[:], in_=bks[:, h, :]), writes=[K_("k")])
                    return q_, k_, l_, c_, S_, Sb_, B_, K_

                def gla_setup(h):
                    q_, k_, l_, c_, S_, Sb_, B_, K_ = gla_head(h)
                    for tq in range(NT // 512):
                        cs4 = slice(tq * 512, (tq + 1) * 512)
                        pz, pzk = ps3.next()
                        mm(S, pz[:], [(w2sb[0:16, h * 128:(h + 1) * 128], bzT[0:16, cs4])], ["w2sb", "bzT"], [pzk])
                        S.op("act", lambda e, pz=pz, cs4=cs4: e.activation(out=l_[:, cs4], in_=pz[:], func=AF.Exp, scale=-1.0, bias=negb[:, h:h + 1]),
                             reads=[pzk, "negb"], writes=[K_("l")])
                        S.op("act", lambda e, cs4=cs4: e.activation(out=l_[:, cs4], in_=l_[:, cs4], func=AF.Ln, bias=smallc[:, 12:13]),
                             reads=[K_("l")], writes=[K_("l")])
                    S.op("pool", lambda e: e.memset(B_[:, 0:1], 0.0), writes=[K_("B")])

                gctx = {}
                psA = SubRing(ps3, [0, 1])
                psU = SubRing(ps3, [2, 3, 4, 5])

                def gla_prep(h, n):
                    q_, k_, l_, c_, S_, Sb_, B_, K_ = gla_bufs(h)
                    cs = slice(n * 128, (n + 1) * 128)
                    S.op("dve", lambda e: e.tensor_tensor_scan(out=c_[:, cs], data0=onesf[:], data1=l_[:, cs], initial=0.0, op0=ALU.mult, op1=ALU.add),
                         reads=[K_("l"), "onesf"], writes=[K_("c")])
                    E0, E0k = e32.next()
                    E1, E1k = e32.next()
                    E3, E3k = e32.next()
                    S.op("act", lambda e: e.activation(out=E0[:], in_=c_[:, cs], func=AF.Exp, scale=-1.0 / 16), reads=[K_("c")], writes=[E0k])
                    S.op("act", lambda e: e.activation(out=E1[:], in_=c_[:, cs], func=AF.Exp, scale=1.0 / 16), reads=[K_("c")], writes=[E1k])
                    S.op("act", lambda e: e.activation(out=E3[:], in_=c_[:, cs], func=AF.Exp, scale=-1.0 / 16, bias=B_[:, n:n + 1]),
                         reads=[K_("c"), K_("B")], writes=[E3k])
                    S.op("dve", lambda e: e.scalar_tensor_tensor(out=B_[:, n + 1:n + 2], in0=c_[:, n * 128 + 127:n * 128 + 128], scalar=-1.0 / 16,
                                                                 in1=B_[:, n:n + 1], op0=ALU.mult, op1=ALU.add),
                         reads=[K_("c"), K_("B")], writes=[K_("B")])
                    qe, qek = b16.next()
                    ki, kik = b16.next()
                    ke, kek = b16.next()
                    S.op("dve", lambda e: e.tensor_tensor(out=qe[:], in0=q_[:, cs], in1=E0[:], op=ALU.mult), reads=[K_("q"), E0k], writes=[qek])
                    S.op("pool", lambda e: e.tensor_tensor(out=ki[:], in0=k_[:, cs], in1=E1[:], op=ALU.mult), reads=[K_("k"), E1k], writes=[kik])
                    S.op("pool", lambda e: e.tensor_scalar(out=ke[:], in0=ki[:], scalar1=E0[:, 127:128], scalar2=None, op0=ALU.mult),
                         reads=[kik, E0k], writes=[kek])
                    S.op("pool", lambda e: e.tensor_tensor(out=qE[:, h, cs], in0=q_[:, cs], in1=E3[:], op=ALU.mult), reads=[K_("q"), E3k], writes=[("qE", h)])
                    ptb, ptbk = pb3.next()
                    S.op("pe", lambda e: e.transpose(ptb[:, 0:128], ke[:], ident_bf[:]), reads=[kek], writes=[ptbk])
                    ket, ketk = b16.next()
                    S.op("act", lambda e: e.activation(out=ket[:], in_=ptb[:, 0:128], func=AF.Copy), reads=[ptbk], writes=[ketk])
                    pA, pAk = psA.next()
                    mm(S, pA[:, 0:128], [(ki[:], qe[:])], [kik, qek], [pAk])
                    Am, Amk = b16.next()
                    S.op("dve", lambda e: e.tensor_tensor(out=Am[:], in0=pA[:, 0:128], in1=maskU[:], op=ALU.mult), reads=[pAk, "maskU"], writes=[Amk])
                    pu, puk = psU.next()
                    mm(S, pu[:, 0:256], [(ket[:], vtok[:, n, h * 256:(h + 1) * 256])], [ketk, ("vtok", n)], [puk])
                    gctx[(h, n)] = (E0, E0k, qe, qek, Am, Amk, pu, puk)

                def gla_rec(h, n):
                    q_, k_, l_, c_, S_, Sb_, B_, K_ = gla_bufs(h)
                    cs = slice(n * 128, (n + 1) * 128)
                    E0, E0k, qe, qek, Am, Amk, pu, puk = gctx.pop((h, n))
                    po, pok = psA.next()

                    def fo(e):
                        last = None
                        for ec in range(2):
                            last = e.matmul(po[:, ec * 128:(ec + 1) * 128], lhsT=vtok[:, n, h * 256 + ec * 128: h * 256 + (ec + 1) * 128], rhs=Am[:],
                                            start=True, stop=(n == 0))
                            if n > 0:
                                last = e.matmul(po[:, ec * 128:(ec + 1) * 128], lhsT=Sb_[:, ec * 128:(ec + 1) * 128], rhs=qe[:], start=False, stop=True)
                        return last
                    S.op("pe", fo, reads=[("vtok", n), Amk, qek, K_("Sb")], writes=[pok])
                    S.op("act", lambda e: e.activation(out=oloc[:, 2 * h:2 * h + 2, cs], in_=po[:, 0:256].rearrange("p (a t) -> p a t", a=2), func=AF.Copy),
                         reads=[pok], writes=[("oloc", h)])
                    if n == 0:
                        S.op("dve", lambda e: e.tensor_copy(out=S_[:], in_=pu[:, 0:256]), reads=[puk], writes=[K_("S")])
                    else:
                        S.op("dve", lambda e: e.scalar_tensor_tensor(out=S_[:], in0=S_[:], scalar=E0[:, 127:128], in1=pu[:, 0:256], op0=ALU.mult, op1=ALU.add),
                             reads=[puk, E0k, K_("S")], writes=[K_("S")])
                    S.op("pool", lambda e: e.tensor_copy(out=Sb_[:], in_=S_[:]), reads=[K_("S")], writes=[K_("Sb")])

                def gla_chunk(h, n):
                    gla_prep(h, n)
                    gla_rec(h, n)

                def gla_export(h):
                    q_, k_, l_, c_, S_, Sb_, B_, K_ = gla_bufs(h)
                    S.op("pool", lambda e: e.tensor_copy(out=gstt[:, h * 264:h * 264 + 256], in_=S_[:]), reads=[K_("S")], writes=["gstt"])
                    S.op("act", lambda e: e.activation(out=gstt[:, h * 264 + 256:h * 264 + 257], in_=B_[:, NT // 128:NT // 128 + 1], func=AF.Exp),
                         reads=[K_("B")], writes=["gstt"])

                S.op("pool", lambda e: e.memset(gstt[:], 0.0), writes=["gstt"])
                for hp in (0, 2):
                    if OPT & 1:
                        gla_setup(hp)
                        gla_setup(hp + 1)
                        NCH = NT // 128
                        gla_prep(hp, 0)
                        gla_prep(hp + 1, 0)
                        for n in range(NCH):
                            if n + 1 < NCH:
                                gla_prep(hp, n + 1)
                                gla_prep(hp + 1, n + 1)
                            gla_rec(hp, n)
                            gla_rec(hp + 1, n)
                        gla_export(hp)
                        gla_export(hp + 1)
                    else:
                        for h_ in (hp, hp + 1):
                            gla_setup(h_)
                            for n in range(NT // 128):
                                gla_chunk(h_, n)
                            gla_export(h_)
                S.dma("sp", lambda e: e.dma_start(out=gst[:, :], in_=gstt[:]), reads=["gstt"], writes=["gst"])
                S.barrier_all()

            with ExitStack() as pa:
                def sba(name, shape, dt):
                    return pa.enter_context(nc.sbuf_tensor("sa_" + name, list(shape), dt))
                ohm = sba("ohm", [128, NCORE, 128], BF16)
                for r in range(NCORE):
                    S.op("dve", lambda e, r=r: e.tensor_scalar(out=ohm[:, r, :], in0=ident[:], scalar1=ohc[:, r:r + 1], scalar2=None, op0=ALU.mult),
                         reads=["ohc"], writes=["ohm"])
                mk = {}
                mpf = sba("mpf", [128, 128], F32)
                S.op("dve", lambda e: e.tensor_scalar(out=mpf[:], in0=mprev[:], scalar1=smallc[:, 5:6], scalar2=None, op0=ALU.mult), reads=["mprev"], writes=["mpf"])
                for name, firsts in (("ff", (1, 1)), ("fn", (1, 0)), ("nn", (0, 0))):
                    mt = sba("mk" + name, [128, 512], BF16)
                    for u in range(2):
                        src = mpf if firsts[u] else mprev
                        S.op("dve", lambda e, mt=mt, u=u, src=src: e.tensor_copy(out=mt[:, u * 128:u * 128 + 128], in_=src[:]), reads=["mpf", "mprev"], writes=[("mk", name)])
                        S.op("dve", lambda e, mt=mt, u=u: e.tensor_copy(out=mt[:, 256 + u * 128:256 + u * 128 + 128], in_=maskU[:]), reads=["maskU"], writes=[("mk", name)])
                    mk[name] = mt
                qa = [sba(f"qa{i}", [128, NT], BF16) for i in range(2)]
                Kc = [sba(f"Kc{i}", [128, 2 * NT], BF16) for i in range(2)]
                Vc = [sba(f"Vc{i}", [128, 2 * NT], BF16) for i in range(2)]
                slots = Ring(pa, nc, "slot", 6, [128, NT], BF16)
                NVT = 69
                Vt = [sba(f"Vt{i}", [128, NVT + 3, 128], BF16) for i in range(1)]
                oacc = sba("oacc", [128, NT], F32)
                dacc = sba("dacc", [128, NT], F32)
                ering = Ring(pa, nc, "er", 3, [128, 512], BF16)
                ohout = Ring(pa, nc, "oho", 1, [128, NT], BF16)
                tiles = []
                for d_ in (1, 4, 16):
                    for r in range(d_):
                        for bb in range(-1, 16 // d_):
                            tiles.append((d_, r, bb))
                tidx = {t: i for i, t in enumerate(tiles)}
                assert len(tiles) == NVT

                def att_head(h):
                    hb = (h % 2) if (OPT & 2) else 0
                    q_, Kc_, Vc_, Vt_ = qa[hb], Kc[hb], Vc[hb], Vt[0]
                    K_ = lambda nm: ("att", nm, (0 if nm == "Vt" else hb))
                    S.dma("sp", lambda e: e.dma_start(out=q_[:], in_=aqs[:, h, :]), writes=[K_("q")])
                    S.dma("sp", lambda e: e.dma_start(out=Kc_[:, NT:2 * NT], in_=kv[:, h * NT:(h + 1) * NT]), writes=[K_("Kown")])
                    S.dma("sp", lambda e: e.dma_start(out=Vc_[:, NT:2 * NT], in_=kv[:, (8 + h) * NT:(9 + h) * NT]), writes=[K_("Vown")])
                    for which, dst, key in ((0, Kc_, K_("Kprev")), (1, Vc_, K_("Vprev"))):
                        banks = [ps3.next() for _ in range(4)]
                        for r in range(NCORE):
                            sl, slk = slots.next()
                            S.dma("sp", lambda e, sl=sl, r=r, which=which: e.dma_start(
                                out=sl[:], in_=kvg[r * 128:(r + 1) * 128, (which * 8 + h) * NT:(which * 8 + h + 1) * NT]), reads=["kvg"], writes=[slk])
                            for ct in range(4):
                                bt, bk = banks[ct]
                                S.op("pe", lambda e, bt=bt, sl=sl, r=r, ct=ct: e.matmul(bt[:], lhsT=ohm[:, r, :], rhs=sl[:, ct * 512:(ct + 1) * 512],
                                                                                     start=(r == 0), stop=(r == NCORE - 1)),
                                     reads=[slk, "ohm"], writes=[bk])
                        for ct in range(4):
                            bt, bk = banks[ct]
                            if ct % 2 == 0:
                                S.op("act", lambda e, bt=bt, ct=ct, dst=dst: e.activation(out=dst[:, ct * 512:(ct + 1) * 512], in_=bt[:], func=AF.Copy), reads=[bk], writes=[key])
                            else:
                                S.op("dve", lambda e, bt=bt, ct=ct, dst=dst: e.tensor_copy(out=dst[:, ct * 512:(ct + 1) * 512], in_=bt[:]), reads=[bk], writes=[key])
                    for t0 in range(0, NVT, 4):
                        ptb, ptbk = pb3.next()
                        grp = tiles[t0:t0 + 4]

                        def trv(e, ptb=ptb, grp=grp):
                            last = None
                            for j, (d_, r, bb) in enumerate(grp):
                                st_ = NT + bb * 128 * d_ + r
                                last = e.transpose(ptb[:, j * 128:(j + 1) * 128], Vc_[:, st_: st_ + 127 * d_ + 1: d_], ident_bf[:])
                            return last
                        S.op("pe", trv, reads=[K_("Vown"), K_("Vprev")], writes=[ptbk])
                        ng = len(grp)
                        if (t0 // 4) % 2 == 0:
                            S.op("act", lambda e, ptb=ptb, t0=t0, ng=ng: e.activation(out=Vt_[:, t0:t0 + ng, :], in_=ptb[:, 0:ng * 128].rearrange("p (a t) -> p a t", a=ng), func=AF.Copy),
                                 reads=[ptbk], writes=[K_("Vt")])
                        else:
                            S.op("dve", lambda e, ptb=ptb, t0=t0, ng=ng: e.tensor_copy(out=Vt_[:, t0:t0 + ng, :], in_=ptb[:, 0:ng * 128].rearrange("p (a t) -> p a t", a=ng)),
                                 reads=[ptbk], writes=[K_("Vt")])
                    S.op("pool", lambda e: e.memset(oacc[:], 0.0), writes=["oacc"])
                    S.op("pool", lambda e: e.memset(dacc[:], 0.0), writes=["dacc"])
                    pairs = []
                    for nb in range(0, 16, 2):
                        pairs.append((1, [(0, nb), (0, nb + 1)], "fn" if nb == 0 else "nn"))
                    for nb in range(4):
                        for r in range(0, 4, 2):
                            pairs.append((4, [(r, nb), (r + 1, nb)], "ff" if nb == 0 else "nn"))
                    for r in range(0, 16, 2):
                        pairs.append((16, [(r, 0), (r + 1, 0)], "ff"))
                    pvq = []

                    def do_pair_S(d_, units, mname):
                        pS, pSk = ps3.next()

                        def fs(e, pS=pS, d_=d_, units=units):
                            last = None
                            for u, (r, nb) in enumerate(units):
                                q0 = nb * 128 * d_ + r
                                qsl = q_[:, q0: q0 + 127 * d_ + 1: d_]
                                for half, bb in enumerate((nb - 1, nb)):
                                    k0 = NT + bb * 128 * d_ + r
                                    last = e.matmul(pS[:, half * 256 + u * 128: half * 256 + (u + 1) * 128], lhsT=Kc_[:, k0: k0 + 127 * d_ + 1: d_], rhs=qsl, start=True, stop=True)
                            return last
                        S.op("pe", fs, reads=[K_("q"), K_("Kown"), K_("Kprev")], writes=[pSk])
                        et, etk = ering.next()
                        S.op("act", lambda e, et=et, pS=pS: e.activation(out=et[:], in_=pS[:], func=AF.Exp, scale=128 ** -0.5), reads=[pSk], writes=[etk])
                        S.op("dve", lambda e, et=et, mname=mname: e.tensor_tensor(out=et[:], in0=et[:], in1=mk[mname][:], op=ALU.mult), reads=[etk, ("mk", mname)], writes=[etk])
                        pvq.append((d_, units, et, etk))

                    def do_pair_PV(d_, units, et, etk):
                        pO, pOk = ps3.next()

                        def fo2(e, pO=pO, et=et, d_=d_, units=units):
                            last = None
                            for u, (r, nb) in enumerate(units):
                                for half, bb in enumerate((nb - 1, nb)):
                                    last = e.matmul(pO[:, u * 128:(u + 1) * 128], lhsT=Vt_[:, tidx[(d_, r, bb)], :], rhs=et[:, half * 256 + u * 128: half * 256 + (u + 1) * 128],
                                                    start=(half == 0), stop=(half == 1))
                            for half in range(2):
                                last = e.matmul(pO[:, 256:512], lhsT=ones_bf[:], rhs=et[:, half * 256:(half + 1) * 256], start=(half == 0), stop=(half == 1))
                            return last
                        S.op("pe", fo2, reads=[etk, K_("Vt")], writes=[pOk])
                        (r0, nb0) = units[0]
                        if d_ == 1:
                            oview = lambda a, nb0=nb0: a[:, nb0 * 128: nb0 * 128 + 256].rearrange("p (u i) -> p u i", u=2)
                        else:
                            oview = lambda a, d_=d_, r0=r0, nb0=nb0: a[:, nb0 * 128 * d_:(nb0 + 1) * 128 * d_].rearrange("p (i r) -> p r i", r=d_)[:, r0:r0 + 2, :]
                        S.op("dve", lambda e, pO=pO, oview=oview: e.tensor_tensor(out=oview(oacc), in0=pO[:, 0:256].rearrange("p (u i) -> p u i", u=2), in1=oview(oacc), op=ALU.add),
                             reads=[pOk, "oacc"], writes=["oacc"])
                        S.op("dve", lambda e, pO=pO, oview=oview: e.tensor_tensor(out=oview(dacc), in0=pO[:, 256:512].rearrange("p (u i) -> p u i", u=2), in1=oview(dacc), op=ALU.add),
                             reads=[pOk, "dacc"], writes=["dacc"])

                    for pi, (d_, units, mname) in enumerate(pairs):
                        do_pair_S(d_, units, mname)
                        if len(pvq) > (1 if (OPT & 2) else 0):
                            do_pair_PV(*pvq.pop(0))
                    while pvq:
                        do_pair_PV(*pvq.pop(0))
                    if OPT & 4:
                        S.op("act", lambda e: e.activation(out=dacc[:], in_=dacc[:], func=AF.Ln), reads=["dacc"], writes=["dacc"])
                        S.op("act", lambda e: e.activation(out=dacc[:], in_=dacc[:], func=AF.Exp, scale=-1.0), reads=["dacc"], writes=["dacc"])
                    else:
                        S.op("dve", lambda e: e.reciprocal(out=dacc[:], in_=dacc[:]), reads=["dacc"], writes=["dacc"])
                    oo, ook = ohout.next()
                    S.op("dve", lambda e, oo=oo: e.tensor_tensor(out=oo[:], in0=oacc[:], in1=dacc[:], op=ALU.mult), reads=["oacc", "dacc"], writes=[ook])
                    S.dma("act", lambda e, oo=oo: e.dma_start(out=oas[:, h, :], in_=oo[:]), reads=[ook], writes=[("oas", h)])

                if "noatt" not in dbg:
                    for h in range(8):
                        att_head(h)
                S.barrier_all()

            if "nocc" not in dbg:
                S.cc(lambda e: e.collective_compute("AllGather", ALU.bypass, replica_groups=rg,
                                                    ins=[gst.opt()], outs=[gstg.opt()]),
                     reads=[], writes=["gstg"])
                S.barrier_all()
            with ExitStack() as pc:
                def sbc(name, shape, dt):
                    return pc.enter_context(nc.sbuf_tensor("sc_" + name, list(shape), dt))
                C.sqring = Ring(pc, nc, "sq3", 3, [128, 2, 512], BF16)
                C.f32ring = Ring(pc, nc, "f32r3", 4, [128, 512], F32)
                gall = sbc("gall", [128, NCORE, GW], F32)
                S.dma("sp", lambda e: e.dma_start(out=gall[:], in_=gstg.rearrange("(r p) w -> p r w", p=128)), reads=["gstg"], writes=["gall"])
                S0 = sbc("S0", [128, 4, 256], F32)
                S0b = sbc("S0b", [128, 4, 256], BF16)
                tmpU = sbc("tmpU", [128, 4, 256], F32)
                aeff = sbc("aeff", [128, 4], F32)
                brt = Ring(pc, nc, "brt", 2, [128, 2, 512], BF16)
                obn = Ring(pc, nc, "obn", 2, [128, 2, 512], BF16)
                S.op("pool", lambda e: e.memset(S0[:], 0.0), writes=["S0"])
                gv = gall[:].rearrange("p r (h w) -> p r h w", h=4)
                for j in range(NCORE):
                    S.op("dve", lambda e, j=j: e.tensor_scalar(out=aeff[:], in0=gv[:, j, :, 256], scalar1=-1.0, scalar2=gmc[:, j:j + 1],
                                                               op0=ALU.add, op1=ALU.mult), reads=["gall", "gmc"], writes=["aeff"])
                    S.op("dve", lambda e: e.tensor_scalar(out=aeff[:], in0=aeff[:], scalar1=1.0, scalar2=None, op0=ALU.add), reads=["aeff"], writes=["aeff"])
                    S.op("dve", lambda e, j=j: e.tensor_scalar(out=tmpU[:], in0=gv[:, j, :, 0:256], scalar1=gmc[:, j:j + 1], scalar2=None, op0=ALU.mult),
                         reads=["gall", "gmc"], writes=["tmpU"])
                    S.op("dve", lambda e: e.tensor_tensor(out=S0[:], in0=S0[:], in1=aeff[:].unsqueeze(2).to_broadcast([128, 4, 256]), op=ALU.mult),
                         reads=["S0", "aeff"], writes=["S0"])
                    S.op("dve", lambda e: e.tensor_tensor(out=S0[:], in0=S0[:], in1=tmpU[:], op=ALU.add), reads=["S0", "tmpU"], writes=["S0"])
                S.op("dve", lambda e: e.tensor_copy(out=S0b[:], in_=S0[:]), reads=["S0"], writes=["S0b"])
                for h in range(4):
                    for tq in range(NT // 512):
                        cs4 = slice(tq * 512, (tq + 1) * 512)
                        for ec in range(2):
                            pcx, pck = ps3.next()
                            mm(S, pcx[:], [(S0b[:, h, ec * 128:(ec + 1) * 128], qE[:, h, cs4])], ["S0b", ("qE", h)], [pck])
                            S.op("dve", lambda e, pcx=pcx, h=h, ec=ec, cs4=cs4: e.tensor_tensor(out=oloc[:, 2 * h + ec, cs4], in0=pcx[:], in1=oloc[:, 2 * h + ec, cs4], op=ALU.add),
                                 reads=[pck, ("oloc", h)], writes=[("oloc", h)])
                        on, onk = obn.next()
                        bt_, btk = brt.next()
                        S.dma("sp", lambda e, bt_=bt_, h=h, cs4=cs4: e.dma_start(out=bt_[:], in_=brs[:, 2 * h:2 * h + 2, cs4]), writes=[btk])
                        rms_feature_major(S, C, smallc[:, 6:8], 2, oloc[:, 2 * h:2 * h + 2, :], lambda k, h=h: ("oloc", h), on, lambda k, onk=onk: onk, 512, tq * 512, dst_n0=0)
                        S.op("pool", lambda e, on=on, bt_=bt_: e.tensor_tensor(out=on[:], in0=on[:], in1=bt_[:], op=ALU.mult), reads=[onk, btk], writes=[onk])
                        S.dma("act", lambda e, on=on, h=h, cs4=cs4: e.dma_start(out=obs[:, 2 * h:2 * h + 2, cs4], in_=on[:]), reads=[onk], writes=[("obs", h, tq)])
            S.barrier_all()

        if "p1only" not in dbg and "nop4" not in dbg:
          with ExitStack() as p4:
            def sb4(name, shape, dt):
                return p4.enter_context(nc.sbuf_tensor("s4_" + name, list(shape), dt))
            xT4 = sb4("xT", [128, KC, T], F32)
            xn4 = sb4("xn", [128, KC, T], BF16)
            C.ps = Ring(p4, nc, "ps4", 8, [128, 512], F32, psum=True)
            C.sqring = Ring(p4, nc, "sq4", 3, [128, 2, 512], BF16)
            C.f32ring = Ring(p4, nc, "f32r4", 4, [128, 512], F32)
            wring4 = Ring(p4, nc, "w4", 5, [128, 4096], BF16)
            hbuf4 = [sb4(f"h{i}", [128, 2, T], BF16) for i in range(2)]
            oab = sb4("oab", [128, 16, 512], BF16)
            gring = Ring(p4, nc, "gr4", 8, [128, 512], BF16)
            ostage = Ring(p4, nc, "ost4", 2, [128, D], F32)

            def wload4(view, a, b):
                wt, wk = wring4.next()
                dst = wt[:, 0:a * b].rearrange("p (a b) -> p a b", a=a)
                S.dma("pool", lambda e: e.dma_start(out=dst, in_=view), writes=[wk])
                return dst, wk
            xkey = lambda k: ("x", k)
            xnkey = lambda k: ("xn4", k)
            E4 = Ctx()
            E4.xT, E4.xn, E4.hbuf, E4.wload, E4.xkey, E4.xnkey = xT4, xn4, hbuf4, wload4, xkey, xnkey
            wav = w_aup.rearrange("(c p) d -> p c d", p=128)
            wbv = w_bup.rearrange("(c p) d -> p c d", p=128)
            wov = w_out.rearrange("(k p) d -> p k d", p=128)

            def tile4(tt):
                tok0 = tt * T
                for k in range(KC):
                    S.dma("act", lambda e, k=k: e.dma_start(out=xT4[:, k, :], in_=x1s[:, k, tok0:tok0 + T]), writes=[xkey(k)])
                for n in range(NH):
                    cs = slice(n * 512, (n + 1) * 512)
                    S.dma("sp", lambda e, cs=cs: e.dma_start(out=oab[:, 0:8, :], in_=oas[:, :, tok0 + cs.start: tok0 + cs.stop]), writes=["oa"])
                    S.dma("sp", lambda e, cs=cs: e.dma_start(out=oab[:, 8:16, :], in_=obs[:, :, tok0 + cs.start: tok0 + cs.stop]), writes=["ob"])
                    for i2 in range(0, KC, 2):
                        wa, wak = wload4(wav[:, :, i2 * 128:(i2 + 2) * 128], 8, 256)
                        wb, wbk = wload4(wbv[:, :, i2 * 128:(i2 + 2) * 128], 8, 256)
                        for j in range(2):
                            i = i2 + j
                            pa_, pak = C.ps.next()
                            pb_, pbk = C.ps.next()
                            mm(S, pa_[:], [(wa[:, c, j * 128:(j + 1) * 128], oab[:, c, :]) for c in range(8)], [wak, "oa"], [pak])
                            mm(S, pb_[:], [(wb[:, c, j * 128:(j + 1) * 128], oab[:, 8 + c, :]) for c in range(8)], [wbk, "ob"], [pbk])
                            ga_, gak = gring.next()
                            gb_, gbk = gring.next()
                            S.dma("sp", lambda e, ga_=ga_, i=i, cs=cs: e.dma_start(out=ga_[:], in_=gas[:, i, tok0 + cs.start: tok0 + cs.stop]), writes=[gak])
                            S.dma("sp", lambda e, gb_=gb_, i=i, cs=cs: e.dma_start(out=gb_[:], in_=gbs[:, i, tok0 + cs.start: tok0 + cs.stop]), writes=[gbk])
                            t1, t1k = C.f32ring.next()
                            t2, t2k = C.f32ring.next()
                            S.op("dve", lambda e, t1=t1, pa_=pa_, ga_=ga_: e.tensor_tensor(out=t1[:], in0=pa_[:], in1=ga_[:], op=ALU.mult), reads=[pak, gak], writes=[t1k])
                            S.op("dve", lambda e, t2=t2, pb_=pb_, gb_=gb_: e.tensor_tensor(out=t2[:], in0=pb_[:], in1=gb_[:], op=ALU.mult), reads=[pbk, gbk], writes=[t2k])
                            S.op("dve", lambda e, t1=t1, t2=t2, i=i, cs=cs: e.tensor_tensor(out=xn4[:, i, cs], in0=t1[:], in1=t2[:], op=ALU.add), reads=[t1k, t2k], writes=[xnkey(i)])
                for i2 in range(0, KC, 2):
                    wo, wok = wload4(wov[:, :, i2 * 128:(i2 + 2) * 128], KC, 256)
                    for j in range(2):
                        i = i2 + j
                        for n in range(NH):
                            cs = slice(n * 512, (n + 1) * 512)
                            py, pyk = C.ps.next()
                            mm(S, py[:], [(wo[:, k, j * 128:(j + 1) * 128], xn4[:, k, cs]) for k in range(KC)], [wok] + [xnkey(k) for k in range(KC)], [pyk])
                            S.op("dve", lambda e, py=py, i=i, cs=cs: e.tensor_tensor(out=xT4[:, i, cs], in0=py[:], in1=xT4[:, i, cs], op=ALU.add), reads=[pyk, xkey(i)], writes=[xkey(i)])
                if debug:
                    for k in range(KC):
                        S.dma("sp", lambda e, k=k: e.dma_start(out=x2s[:, k, tok0:tok0 + T], in_=xT4[:, k, :]), reads=[xkey(k)], writes=[("x2s", k, tt)])
                for n in range(NH):
                    rms_feature_major(S, C, gcols[:, 2 * KC:3 * KC], KC, xT4, xkey, xn4, xnkey, 512, n * 512)
                ffn_block(S, C, E4, w_g2, w_u2, w_d2)
                for sub in range(T // 128):
                    og, ogk = ostage.next()
                    for k4 in range(0, KC, 4):
                        pt, pk = C.ps.next()

                        def tr(e, pt=pt, k4=k4, sub=sub):
                            last = None
                            for j in range(4):
                                last = e.transpose(pt[:, j * 128:(j + 1) * 128], xT4[:, k4 + j, sub * 128:(sub + 1) * 128], ident[:])
                            return last
                        S.op("pe", tr, reads=[xkey(k4 + j) for j in range(4)], writes=[pk])
                        if (k4 // 4) % 2 == 0:
                            S.op("act", lambda e, pt=pt, og=og, k4=k4: e.activation(out=og[:, k4 * 128:(k4 + 4) * 128], in_=pt[:], func=AF.Copy), reads=[pk], writes=[ogk])
                        else:
                            S.op("dve", lambda e, pt=pt, og=og, k4=k4: e.tensor_copy(out=og[:, k4 * 128:(k4 + 4) * 128], in_=pt[:]), reads=[pk], writes=[ogk])
                    S.dma("sp", lambda e, og=og, sub=sub: e.dma_start(out=out[tok0 + sub * 128: tok0 + (sub + 1) * 128, :], in_=og[:]), reads=[ogk], writes=[("out", tt, sub)])

            for tt in range(NTT):
                tile4(tt)
        S.emit(st)
    return nc


def _consts():
    ident = np.eye(128, dtype=np.float32)
    ropeP = np.zeros((128, 128), np.float32)
    for m in range(16):
        ropeP[m + 16, m] = 1.0
    for m in range(16, 32):
        ropeP[m - 16, m] = 1.0
    half = 16
    inv = np.power(np.float32(500000.0), -(np.arange(half, dtype=np.float32) * np.float32(2.0) / np.float32(32))).astype(np.float32)
    inv_col = np.zeros((128, 1), np.float32)
    inv_col[0:16, 0] = inv
    inv_col[16:32, 0] = inv
    sgn = np.zeros((128, 1), np.float32)
    sgn[0:16] = -1.0
    sgn[16:32] = 1.0
    kk = np.arange(128)[:, None]
    qq = np.arange(128)[None, :]
    mprev = (kk >= qq).astype(np.float32)
    mcur = (kk <= qq).astype(np.float32)
    return dict(ident=ident, ropeP=ropeP, inv_col=inv_col, sgn_col=sgn, mprev=mprev, mcur=mcur)


def make_in_maps(inputs):
    f = lambda a: np.ascontiguousarray(np.asarray(a))
    x = f(inputs["x"])[0]
    pos = f(inputs["positions"])
    col = lambda v: np.ascontiguousarray(f(v)[0].reshape(-1, 128).T)
    common = dict(
        ffn1_w_gate=f(inputs["ffn1_w_gate"])[0], ffn1_w_up=f(inputs["ffn1_w_up"])[0], ffn1_w_down=f(inputs["ffn1_w_down"])[0],
        ffn2_w_gate=f(inputs["ffn2_w_gate"])[0], ffn2_w_up=f(inputs["ffn2_w_up"])[0], ffn2_w_down=f(inputs["ffn2_w_down"])[0],
        w_in=f(inputs["w_in"])[0], w_a_up=f(inputs["w_a_up"])[0], w_b_up=f(inputs["w_b_up"])[0], w_out=f(inputs["w_out"])[0],
        g_ffn1=col(inputs["ffn1_norm"]), g_mix=col(inputs["mix_norm"]), g_ffn2=col(inputs["ffn2_norm"]),
        g_aq=col(inputs["a_q_norm"]), g_ak=col(inputs["a_k_norm"]),
        gate_w2=f(inputs["b_gate_w2"])[0], gate_b=col(inputs["b_gate_bias"]), g_bout=col(inputs["b_out_norm"]),
    )
    common.update(_consts())
    maps = []
    for c in range(NCORE):
        m = dict(common)
        m["x"] = np.ascontiguousarray(x[c * NT:(c + 1) * NT])
        m["pos"] = np.ascontiguousarray(pos[:, c * NT:(c + 1) * NT]).astype(np.int32)
        oh = np.zeros((128, NCORE), np.float32)
        if c > 0:
            oh[:, c - 1] = 1.0
        m["onehot"] = oh
        m["hasprev"] = np.full((128, 1), 1.0 if c > 0 else 0.0, np.float32)
        gm = np.zeros((128, NCORE), np.float32)
        gm[:, :c] = 1.0
        m["gmask"] = gm
        maps.append(m)
    return maps


def kernel(**inputs):
    nc = build_program(DEBUG)
    maps = make_in_maps(inputs)
    res = run_bass_kernel_spmd(nc, maps, core_ids=list(range(NCORE)))
    if DEBUG:
        return res
    outp = np.concatenate([res.results[c]["out"] for c in range(NCORE)], axis=0)
    return outp.reshape(1, SEQ, D).astype(np.float32)
```
